# Optimizing a Trainium2 kernel written in Bass

```python
import math
import jax, jax.numpy as jnp
from jax import lax
import numpy as np

D_MODEL = 1024
BATCH = 8
SEQ = 2048
DEPTH = 2

HEAD_DIM = 64
BLOCK_Q = 128
N_SB_HEADS = 4
DIL_PATTERNS = ((128, 1), (512, 4), (2048, 16))
N_DIL_GROUPS = len(DIL_PATTERNS)
N_DIL_HEADS = 4
N_FOX_HEADS = 4
N_DIFF_HEADS = 4
DIFF_QK_DIM = HEAD_DIM // 2
N_BRANCH = 4
BRANCH_WIDTH = 4 * HEAD_DIM
D_FF = 4 * D_MODEL
RMS_EPS = 1e-6
N_ALIBI = N_DIL_GROUPS * N_DIL_HEADS + N_DIFF_HEADS

SB_W = N_SB_HEADS * HEAD_DIM
DIL_W = N_DIL_GROUPS * N_DIL_HEADS * HEAD_DIM
FOX_W = N_FOX_HEADS * HEAD_DIM
DIFF_QK_W = N_DIFF_HEADS * 2 * DIFF_QK_DIM
DIFF_V_W = N_DIFF_HEADS * HEAD_DIM
GATE_W = N_BRANCH * D_MODEL
SPLIT_SIZES = (SB_W,) * 3 + (DIL_W,) * 3 + (FOX_W,) * 3 + (N_FOX_HEADS,) + (DIFF_QK_W, DIFF_QK_W, DIFF_V_W) + (GATE_W,)
D_IN = sum(SPLIT_SIZES)
SPLIT_OFFSETS = tuple(sum(SPLIT_SIZES[:i + 1]) for i in range(len(SPLIT_SIZES) - 1))

kernel_name = 'hybrid_gated_four_mixer_decoder'


def rms_norm(x, g):
    xf = x.astype(jnp.float32)
    y = xf * lax.rsqrt(jnp.mean(xf * xf, axis=-1, keepdims=True) + RMS_EPS)
    return (y * g.astype(jnp.float32)).astype(x.dtype)


def alibi_slopes():
    return jnp.asarray(2.0 ** (-8.0 * np.arange(1, N_ALIBI + 1) / N_ALIBI), dtype=jnp.float32)


def _stick_breaking(q_blk, k, v, dist):
    z = jnp.einsum('bqhd,bshd->bhqs', q_blk, k).astype(jnp.float32) * (HEAD_DIM ** -0.5)
    strict = dist > 0
    log_keep = jnp.where(strict, jax.nn.log_sigmoid(-z), 0.0)
    log_between = lax.cumsum(log_keep, axis=3, reverse=True) - log_keep
    a = jnp.where(strict, jnp.exp(jax.nn.log_sigmoid(z) + log_between), 0.0)
    return jnp.einsum('bhqs,bshd->bqhd', a.astype(v.dtype), v)


def _dilated(q_blk, k, v, t_idx, slopes):
    outs, lses = [], []
    for g, (window, dil) in enumerate(DIL_PATTERNS):
        dists = dil * jnp.arange(window // dil + 1)
        idx = t_idx[:, None] - dists[None, :]
        valid = idx >= 0
        idx = jnp.maximum(idx, 0)
        kg = jnp.take(k[:, :, g], idx, axis=1)
        vg = jnp.take(v[:, :, g], idx, axis=1)
        s = jnp.einsum('bqhd,bqnhd->bhqn', q_blk[:, :, g], kg).astype(jnp.float32) * (HEAD_DIM ** -0.5)
        s = s - slopes[g][:, None, None] * dists.astype(jnp.float32)
        s = jnp.where(valid, s, -jnp.inf)
        lse = jax.nn.logsumexp(s, axis=-1)
        p = jnp.exp(s - lse[..., None])
        outs.append(jnp.einsum('bhqn,bqnhd->bqhd', p.astype(v.dtype), vg))
        lses.append(lse)
    alpha = jax.nn.softmax(jnp.stack(lses, axis=0), axis=0)
    alpha = jnp.transpose(alpha, (0, 1, 3, 2))[..., None]
    return jnp.sum(alpha.astype(v.dtype) * jnp.stack(outs, axis=0), axis=0)


def _forgetting(q_blk, k, v, cum_blk, cum, dist):
    s = jnp.einsum('bqhd,bshd->bhqs', q_blk, k).astype(jnp.float32) * (HEAD_DIM ** -0.5)
    s = s + jnp.transpose(cum_blk, (0, 2, 1))[..., None] - jnp.transpose(cum, (0, 2, 1))[:, :, None, :]
    p = jax.nn.softmax(jnp.where(dist >= 0, s, -jnp.inf), axis=-1)
    return jnp.einsum('bhqs,bshd->bqhd', p.astype(v.dtype), v)


def _differential(q_blk, k, v, dist, slopes, lam):
    s = jnp.einsum('bqhcd,bshcd->bchqs', q_blk, k).astype(jnp.float32) * (DIFF_QK_DIM ** -0.5)
    s = s - slopes[:, None, None] * dist.astype(jnp.float32)
    p = jax.nn.softmax(jnp.where(dist >= 0, s, -jnp.inf), axis=-1)
    w = p[:, 0] - lam * p[:, 1]
    return jnp.einsum('bhqs,bshd->bqhd', w.astype(v.dtype), v)


def _token_mixers(h, w_in, b_forget, lam, diff_g, diff_out_scale, slopes_dil, slopes_diff):
    bsz, seq, _ = h.shape
    proj = h @ w_in
    (qa, ka, va, qb, kb, vb, qc, kc, vc, f_logit, qd, kd, vd, gate_logit) = jnp.split(proj, SPLIT_OFFSETS, axis=-1)
    qa, ka, va = (a.reshape(bsz, seq, N_SB_HEADS, HEAD_DIM) for a in (qa, ka, va))
    qb, kb, vb = (a.reshape(bsz, seq, N_DIL_GROUPS, N_DIL_HEADS, HEAD_DIM) for a in (qb, kb, vb))
    qc, kc, vc = (a.reshape(bsz, seq, N_FOX_HEADS, HEAD_DIM) for a in (qc, kc, vc))
    qd, kd = (a.reshape(bsz, seq, N_DIFF_HEADS, 2, DIFF_QK_DIM) for a in (qd, kd))
    vd = vd.reshape(bsz, seq, N_DIFF_HEADS, HEAD_DIM)
    log_f = jax.nn.log_sigmoid(f_logit.astype(jnp.float32) + b_forget.astype(jnp.float32))
    cum = jnp.cumsum(log_f, axis=1)
    s_idx = jnp.arange(seq)

    def block(q0):
        t_idx = q0 + jnp.arange(BLOCK_Q)
        dist = t_idx[:, None] - s_idx[None, :]
        sl = lambda a: lax.dynamic_slice_in_dim(a, q0, BLOCK_Q, axis=1)
        return (_stick_breaking(sl(qa), ka, va, dist),
                _dilated(sl(qb), kb, vb, t_idx, slopes_dil),
                _forgetting(sl(qc), kc, vc, sl(cum), cum, dist),
                _differential(sl(qd), kd, vd, dist, slopes_diff, lam))

    o_a, o_b, o_c, o_d = lax.map(block, jnp.arange(seq // BLOCK_Q) * BLOCK_Q)
    unblock = lambda o: jnp.moveaxis(o, 0, 1).reshape(bsz, seq, o.shape[3], o.shape[4])
    o_d = rms_norm(unblock(o_d), diff_g) * diff_out_scale
    ys = jnp.stack([unblock(o_a), unblock(o_b), unblock(o_c), o_d], axis=2)
    return ys.reshape(bsz, seq, N_BRANCH, BRANCH_WIDTH), gate_logit


def setup_inputs(seed: int = 0) -> dict:
    key = jax.random.key(seed)
    ks = jax.random.split(key, 16)
    nrm = lambda k, shape, scale: jax.random.normal(k, shape, jnp.float32) * scale
    return {
        'x': nrm(ks[0], (BATCH, SEQ, D_MODEL), 1.0),
        'mix_norm_g': 1.0 + nrm(ks[1], (DEPTH, D_MODEL), 0.05),
        'w_in': nrm(ks[2], (DEPTH, D_MODEL, D_IN), D_MODEL ** -0.5),
        'b_forget': jax.random.uniform(ks[3], (DEPTH, N_FOX_HEADS), jnp.float32, 1.0, 4.0),
        'lambda_q1': nrm(ks[4], (DEPTH, DIFF_QK_DIM), 0.1),
        'lambda_k1': nrm(ks[5], (DEPTH, DIFF_QK_DIM), 0.1),
        'lambda_q2': nrm(ks[6], (DEPTH, DIFF_QK_DIM), 0.1),
        'lambda_k2': nrm(ks[7], (DEPTH, DIFF_QK_DIM), 0.1),
        'diff_norm_g': 1.0 + nrm(ks[8], (DEPTH, HEAD_DIM), 0.05),
        'w_branch': nrm(ks[9], (DEPTH, N_BRANCH, BRANCH_WIDTH, D_MODEL), BRANCH_WIDTH ** -0.5),
        'w_out': nrm(ks[10], (DEPTH, D_MODEL, D_MODEL), D_MODEL ** -0.5),
        'mlp_norm_g': 1.0 + nrm(ks[11], (DEPTH, D_MODEL), 0.05),
        'w_up': nrm(ks[12], (DEPTH, D_MODEL, D_FF), D_MODEL ** -0.5),
        'w_down': nrm(ks[13], (DEPTH, D_FF, D_MODEL), D_FF ** -0.5),
        'final_norm_g': 1.0 + nrm(ks[14], (D_MODEL,), 0.05),
    }


def reference(x, mix_norm_g, w_in, b_forget, lambda_q1, lambda_k1, lambda_q2, lambda_k2, diff_norm_g, w_branch, w_out, mlp_norm_g, w_up, w_down, final_norm_g):
    bsz, seq, _ = x.shape
    slopes = alibi_slopes()
    n0, n1 = N_DIL_HEADS, N_DIL_HEADS + N_DIFF_HEADS
    slopes_dil = jnp.stack([slopes[:n0], slopes[n1:n1 + N_DIL_HEADS], slopes[n1 + N_DIL_HEADS:]], axis=0)
    slopes_diff = slopes[n0:n1]
    for l in range(DEPTH):
        lambda_init = 0.8 - 0.6 * math.exp(-0.3 * l)
        lam = (jnp.exp(jnp.sum(lambda_q1[l].astype(jnp.float32) * lambda_k1[l].astype(jnp.float32)))
               - jnp.exp(jnp.sum(lambda_q2[l].astype(jnp.float32) * lambda_k2[l].astype(jnp.float32))) + lambda_init)
        h = rms_norm(x, mix_norm_g[l])
        ys, gate_logit = _token_mixers(h, w_in[l], b_forget[l], lam, diff_norm_g[l], 1.0 - lambda_init, slopes_dil, slopes_diff)
        branch = jnp.einsum('bsnc,ncd->bsnd', ys, w_branch[l])
        gates = jax.nn.sigmoid(gate_logit.reshape(bsz, seq, N_BRANCH, D_MODEL))
        x = x + jnp.sum(gates * branch, axis=2) @ w_out[l]
        h = rms_norm(x, mlp_norm_g[l])
        x = x + jnp.square(jax.nn.relu(h @ w_up[l])) @ w_down[l]
    return rms_norm(x, final_norm_g)
```

```python
import math
from contextlib import ExitStack
import numpy as np
import concourse.bass as bass
import concourse.mybir as mybir
from concourse.bass_utils import run_bass_kernel_spmd

F32 = mybir.dt.float32
BF16 = mybir.dt.bfloat16
AF = mybir.ActivationFunctionType
ALU = mybir.AluOpType
AX = mybir.AxisListType

S_LEN = 2048
D = 1024
NCH = 8
NTT = 4
NKB = 16
D_IN = 8708
DEPTH = 2
EPS = 1e-6
BIG = 30000.0
NWB = 6

OFF_QA, OFF_KA, OFF_VA = 0, 256, 512
OFF_QB, OFF_KB, OFF_VB = 768, 768 + 768, 768 + 1536
OFF_QC, OFF_KC, OFF_VC = 3072, 3328, 3584
OFF_F = 3840
OFF_QD, OFF_KD, OFF_VD = 3844, 4100, 4356
OFF_G = 4612

_SL = 2.0 ** (-8.0 * np.arange(1, 17) / 16.0)
SLOPES_DIL = np.stack([_SL[0:4], _SL[8:12], _SL[12:16]], 0)
SLOPES_DIFF = _SL[4:8]
DIL_R = (1, 4, 16)

CF_ONES = 0
CF_TRI = 384
CF_SEL127 = 512
CF_NEGM = 640
CF_DB = 768
CF_M0 = 768 + 64
CF_N = 768 + 68
CB_ONES = 0
CB_NTRI = 128
CB_NONES = 256
CB_BD = 384
CB_ZERO = 512
CB_TRII = 640
CB_TRIS = 768
CB_DW = 896
CB_N = 896 + 256


def _host_consts():
    p = np.arange(128)[:, None].astype(np.float64)
    c = np.arange(128)[None, :].astype(np.float64)
    cf = np.zeros((128, CF_N), np.float32)
    cf[:, CF_ONES:CF_ONES + 384] = 1.0
    cf[:, CF_TRI:CF_TRI + 128] = (p <= c)
    cf[:, CF_SEL127:CF_SEL127 + 128] = (p == 127)
    cf[:, CF_NEGM:CF_NEGM + 128] = np.where(c >= p, 0.0, -BIG)
    for h in range(4):
        for mm in range(16):
            cf[:, CF_DB + h * 16 + mm] = SLOPES_DIFF[h] * (p[:, 0] + 128.0 * (mm - 12) - 256.0)
    for k in range(4):
        cf[:, CF_M0 + k] = ((p[:, 0] // 32) == k)
    cb = np.zeros((128, CB_N), np.float32)
    cb[:, CB_ONES:CB_ONES + 128] = 1.0
    cb[:, CB_NTRI:CB_NTRI + 128] = np.where(p >= c, -1.0, 0.0)
    cb[:, CB_NONES:CB_NONES + 128] = -1.0
    cb[:, CB_BD:CB_BD + 128] = ((p // 64) == (c // 64))
    cb[:, CB_TRII:CB_TRII + 128] = (c >= p)
    cb[:, CB_TRIS:CB_TRIS + 128] = (c > p)
    x = np.arange(256)[None, :].astype(np.float64)
    dd = x - p
    cb[:, CB_DW:CB_DW + 256] = np.where((dd >= 0) & (dd <= 128), dd, BIG)
    ident = np.eye(128, dtype=np.float32)
    return cf, cb, ident


class Sched:
    CE = ('pe', 'act', 'dve', 'pool')

    def __init__(self):
        self.streams = {e: [] for e in ('pe', 'act', 'dve', 'pool', 'sp')}
        self.cnt = {e: 0 for e in self.CE}
        self.dcnt = {}
        self.lastw = {}
        self.readers = {}
        self.seen = {e: {} for e in self.streams}

    def _waits(self, eng, reads, writes):
        best = {}

        def add(tok):
            k, v = tok
            if k == 'pe' and eng == 'pe':
                return
            if best.get(k, 0) < v:
                best[k] = v
        for r in reads:
            t = self.lastw.get(r)
            if t:
                add(t)
            if isinstance(r, tuple) and r[0] == 'ps':
                for k, v in self.readers.get(r, {}).items():
                    if k != eng:
                        add((k, v))
        for w in writes:
            t = self.lastw.get(w)
            if t:
                add(t)
            for k, v in self.readers.get(w, {}).items():
                add((k, v))
        out = []
        for k, v in best.items():
            if self.seen[eng].get(k, 0) >= v:
                continue
            self.seen[eng][k] = v
            out.append((k, v))
        return out

    def _reg(self, tok, reads, writes):
        k, v = tok
        for r in reads:
            d = self.readers.setdefault(r, {})
            if d.get(k, 0) < v:
                d[k] = v
        for w in writes:
            self.lastw[w] = tok
            self.readers[w] = {}

    def op(self, eng, fn, reads=(), writes=()):
        waits = self._waits(eng, reads, writes)
        self.cnt[eng] += 1
        tok = (eng, self.cnt[eng])
        self.streams[eng].append((fn, waits, eng, 1))
        self._reg(tok, reads, writes)

    def dma(self, q, sem, fn, reads=(), writes=()):
        waits = self._waits(q, reads, writes)
        prev = self.dcnt.get(sem, 0)
        if prev > 0 and self.seen[q].get(sem, 0) < prev:
            self.seen[q][sem] = prev
            waits.append((sem, prev))
        self.dcnt[sem] = prev + 16
        tok = (sem, self.dcnt[sem])
        self.streams[q].append((fn, waits, sem, 16))
        self._reg(tok, reads, writes)


MMLOG = []


class _Stop(Exception):
    pass


def build_program(layers, final_norm, dbg=None, stop_stage=None, wspecs=None):
    del MMLOG[:]
    stage = [0]

    def chk(name):
        stage[0] += 1
        if stop_stage is not None and stage[0] >= stop_stage:
            print('STOP at stage', stage[0], name)
            raise _Stop()
    nc = bass.Bass("TRN2", target_bir_lowering=False, dynamic_dma_scratch_size=8192)
    S = Sched()

    def din(name, shape):
        return nc.dram_tensor(name, list(shape), F32, kind="ExternalInput").ap()
    x_d = din("x", [S_LEN, D])
    w_in_d = din("w_in", [DEPTH, D, D_IN])
    w_br_d = din("w_branch", [DEPTH, 4, 256, D])
    w_out_d = din("w_out", [DEPTH, D, D])
    w_up_d = din("w_up", [DEPTH, D, 4 * D])
    w_dn_d = din("w_down", [DEPTH, 4 * D, D])
    g_mix_d = din("mix_norm_g", [DEPTH, D])
    g_mlp_d = din("mlp_norm_g", [DEPTH, D])
    g_fin_d = din("final_norm_g", [1, D])
    bf_d = din("b_forget", [DEPTH, 4])
    lq1_d = din("lambda_q1", [DEPTH, 32])
    lk1_d = din("lambda_k1", [DEPTH, 32])
    lq2_d = din("lambda_q2", [DEPTH, 32])
    lk2_d = din("lambda_k2", [DEPTH, 32])
    gd_d = din("diff_norm_g", [DEPTH, 64])
    cf_d = din("cf", [128, CF_N])
    cb_d = din("cb", [128, CB_N])
    id_d = din("ident", [128, 128])
    y_d = nc.dram_tensor("y", [S_LEN, D], F32, kind="ExternalOutput").ap()
    dbg_d = None
    if dbg:
        dbg_d = nc.dram_tensor("dbg", [128, 8 * S_LEN], F32, kind="ExternalOutput").ap()

    es = ExitStack()
    with es:
        def sb(name, shape, dt):
            return es.enter_context(nc.sbuf_tensor(name, list(shape), dt))
        xT = sb("xT", [128, NCH, S_LEN], F32)
        hT = sb("hT", [128, NCH, S_LEN], BF16)
        ARA = sb("ARA", [128, 8192], F32)
        ysT = ARA[:, :].bitcast(BF16).rearrange("p (c t) -> p c t", c=NCH)
        accB = ARA[:, 4096:8192].rearrange("p (h t) -> p h t", h=2)
        stg = ARA[:, 0:2048].rearrange("p (b f) -> p b f", b=2)
        ARU = sb("ARU", [128, 8192], BF16)
        qT = ARU[:, 0:2048]
        kT = ARU[:, 2048:4096]
        vA = ARU[:, 4096:8192].rearrange("p (j h e) -> p j h e", j=NKB, h=2)
        MB = ARU[:, :].rearrange("p (c t) -> p c t", c=4)
        q2T = sb("q2T", [128, S_LEN], BF16)
        q3T = sb("q3T", [128, S_LEN], BF16)
        q4T = sb("q4T", [128, S_LEN], BF16)
        wbuf = [sb("wb%d" % i, [128, 1024], BF16) for i in range(NWB)]
        NWF, NWH = 6, 6
        WF = [sb("wf%d" % i, [128, 512], F32) for i in range(NWF)]
        WH = [sb("wh%d" % i, [128, 512], BF16) for i in range(NWH)]
        cf = sb("cf_s", [128, CF_N], F32)
        cb = sb("cb_s", [128, CB_N], BF16)
        ident = sb("ident_s", [128, 128], F32)
        gT = sb("gT", [128, 5, NCH], F32)
        ltab = sb("ltab", [128, 4, 32], F32)
        lsm = sb("lsm", [128, 16], F32)
        bfb = sb("bfb", [128, 4], F32)
        logf = sb("logf", [128, NKB, 4], F32)
        cumtok = sb("cumtok", [128, NKB, 4], F32)
        cend = sb("cend", [128, NKB, 4], F32)
        FB = sb("FB", [128, NTT, NKB, 4], F32)
        CB2 = sb("CB2", [128, 512], F32)
        clr = sb("clr", [1, 512], F32)
        PS = [es.enter_context(nc.psum_tensor("ps%d" % i, [128, 512], F32)) for i in range(8)]

        sem_names = ['pe', 'act', 'dve', 'pool'] + ['w%d' % i for i in range(NWB)] + ['ld', 'ldg', 'ld0', 'ld1', 'st0', 'st1', 'dbg']
        sems = {n: es.enter_context(nc.semaphore(n)) for n in sem_names}

        ENG = {'pe': nc.tensor, 'act': nc.scalar, 'dve': nc.vector, 'pool': nc.gpsimd, 'sp': nc.sync}
        wfc = [0]
        whc = [0]

        def nwf():
            wfc[0] += 1
            i = wfc[0] % NWF
            return WF[i], ('wf', i)

        def nwh():
            whc[0] += 1
            i = whc[0] % NWH
            return WH[i], ('wh', i)

        def mm(out, lhsT, rhs, start, stop, reads, writes, sgc=False):
            if sgc:
                S.op('pe', lambda: nc.tensor.matmul(out, lhsT=lhsT, rhs=rhs, start=start, stop=stop, skip_group_check=True), reads, writes)
            else:
                S.op('pe', lambda: nc.tensor.matmul(out, lhsT=lhsT, rhs=rhs, start=start, stop=stop), reads, writes)

        def act(out, in_, func, reads, writes, bias=None, scale=None):
            kw = {}
            if bias is not None:
                kw['bias'] = bias
            if scale is not None:
                kw['scale'] = scale
            S.op('act', lambda: nc.scalar.activation(out=out, in_=in_, func=func, **kw), reads, writes)

        def tt(eng, out, in0, in1, op, reads, writes):
            S.op(eng, lambda: ENG[eng].tensor_tensor(out=out, in0=in0, in1=in1, op=op), reads, writes)

        def stt(eng, out, in0, scalar, in1, op0, op1, reads, writes):
            S.op(eng, lambda: ENG[eng].scalar_tensor_tensor(out=out, in0=in0, scalar=scalar, in1=in1, op0=op0, op1=op1), reads, writes)

        def ts(eng, out, in0, s1, s2, op0, op1, reads, writes):
            if s2 is None:
                S.op(eng, lambda: ENG[eng].tensor_scalar(out=out, in0=in0, scalar1=s1, scalar2=None, op0=op0), reads, writes)
            else:
                S.op(eng, lambda: ENG[eng].tensor_scalar(out=out, in0=in0, scalar1=s1, scalar2=s2, op0=op0, op1=op1), reads, writes)

        def cp(eng, out, in_, reads, writes):
            if eng == 'act':
                S.op('act', lambda: nc.scalar.copy(out=out, in_=in_), reads, writes)
            else:
                S.op(eng, lambda: ENG[eng].tensor_copy(out=out, in_=in_), reads, writes)

        def recip_dve(out, in_, reads, writes):
            S.op('dve', lambda: nc.vector.reciprocal(out=out, in_=in_), reads, writes)

        def recip(out, in_, reads, writes):
            act(out, in_, AF.Ln, reads, writes)
            act(out, out, AF.Exp, writes, writes, scale=-1.0)

        def memset(eng, ap, val, writes):
            S.op(eng, lambda: ENG[eng].memset(ap, val), (), writes)

        DR = {'w_in': w_in_d, 'w_branch': w_br_d, 'w_out': w_out_d, 'w_up': w_up_d, 'w_down': w_dn_d}
        wcount = [0]
        wissued = [0]
        recorded = []
        WPF = 3

        def _wissue(k, spec):
            tname, idx, r0, r1, c0, c1 = spec
            kc = (r1 - r0) // 128
            ncols = c1 - c0
            i = k % NWB
            dst = wbuf[i][:, 0:kc * 128].rearrange("p (c n) -> p c n", c=kc)[:, :, 0:ncols]
            src = DR[tname][tuple(idx) + (slice(r0, r1), slice(c0, c1))].rearrange("(c p) n -> p c n", p=128)
            S.dma('pool', 'w%d' % i, lambda: nc.gpsimd.dma_start(out=dst, in_=src), (), [('w', i)])

        def wload(tname, idx, r0, r1, c0, c1):
            spec = (tname, tuple(idx), r0, r1, c0, c1)
            k = wcount[0]
            wcount[0] += 1
            if wspecs is None:
                recorded.append(spec)
                _wissue(k, spec)
            else:
                assert wspecs[k] == spec, (k, spec, wspecs[k])
                while wissued[0] <= min(k + WPF, len(wspecs) - 1):
                    _wissue(wissued[0], wspecs[wissued[0]])
                    wissued[0] += 1
            kc = (r1 - r0) // 128
            i = k % NWB
            return wbuf[i][:, 0:kc * 128].rearrange("p (c n) -> p c n", c=kc), ('w', i)

        S.dma('sp', 'ld', lambda: nc.sync.dma_start(out=cf[:, :], in_=cf_d[:, :]), (), ['cf'])
        S.dma('sp', 'ld', lambda: nc.sync.dma_start(out=ident[:, :], in_=id_d[:, :]), (), ['ident'])
        S.dma('pool', 'w0', lambda: nc.gpsimd.dma_start(out=cb[:, :], in_=cb_d[:, :]), (), ['cb'])
        gsrcs = [g_mix_d[0:1, :], g_mlp_d[0:1, :], g_mix_d[1:2, :], g_mlp_d[1:2, :], g_fin_d[0:1, :]]
        for gi, gs in enumerate(gsrcs):
            src = gs.rearrange("o (c p) -> p (o c)", p=128)
            S.dma('act', 'ldg', (lambda gi=gi, src=src: nc.scalar.dma_start(out=gT[:, gi, :], in_=src, allow_slow_non_contiguous=True)), (), ['gT'])

        xres = lambda c, i: ('xT', c, i)
        stgres = lambda sl: [('ys', sl, i) for i in range(NTT)]
        ares = lambda hh: [('accB', hh)] + [('ys', c, i) for c in (4 + 2 * hh, 5 + 2 * hh) for i in range(NTT)]

        def mbres(dcc, i):
            if dcc == 0:
                return [('q', i)]
            if dcc == 1:
                return [('k', i)]
            if dcc == 2:
                return [('v', i // 2), 'vones']
            return [('v', 2 + i // 2), 'vones']
        for b in range(NKB):
            sl = b % 2
            S.dma('sp', 'ld%d' % sl, (lambda b=b, sl=sl: nc.sync.dma_start(out=stg[:, sl, :], in_=x_d[128 * b:128 * b + 128, :])),
                  (), stgres(sl))
            for half in range(2):
                bank = (2 * b + half) % 4
                for cc in range(4):
                    c = 4 * half + cc
                    S.op('pe', (lambda bank=bank, cc=cc, c=c, sl=sl: nc.tensor.transpose(
                        out=PS[bank][:, 128 * cc:128 * cc + 128], in_=stg[:, sl, 128 * c:128 * c + 128], identity=ident[:, :])),
                        stgres(sl) + ['ident'], [('ps', bank)])
                eng = 'dve' if half == 0 else 'act'
                cp(eng, xT[:, 4 * half:4 * half + 4, 128 * b:128 * b + 128],
                   PS[bank][:, :].rearrange("p (a t) -> p a t", a=4),
                   [('ps', bank)], [xres(c, b // 4) for c in range(4 * half, 4 * half + 4)])

        def rmsnorm_to_hT(gidx):
            for i in range(NTT):
                tsl = slice(512 * i, 512 * i + 512)
                for c in range(NCH):
                    sq, sqr = nwh()
                    act(sq[:, :], xT[:, c, tsl], AF.Square, [xres(c, i)], [sqr])
                    mm(PS[7][:, :], cb[:, CB_ONES:CB_ONES + 128], sq[:, :], c == 0, c == NCH - 1, [sqr, 'cb'], [('ps', 7)])
                lnv, lnr = nwf()
                act(lnv[:, :], PS[7][:, :], AF.Ln, [('ps', 7)], [lnr], bias=EPS, scale=1.0 / D)
                rs, rsr = nwf()
                act(rs[:, :], lnv[:, :], AF.Exp, [lnr], [rsr], scale=-0.5)
                for c in range(NCH):
                    eng = 'dve'
                    stt(eng, hT[:, c, tsl], xT[:, c, tsl], gT[:, gidx, c:c + 1], rs[:, :], ALU.mult, ALU.mult,
                        [xres(c, i), rsr, 'gT'], [('hT', c, i)])

        hres_all = lambda i: [('hT', c, i) for c in range(NCH)]

        def project_unit(l, qoff, koff, voff, r, qscale, dmode=False):
            wq3, wqr = wload('w_in', (l,), 0, D, qoff, qoff + 128)
            wk3, wkr = wload('w_in', (l,), 0, D, koff, koff + 128)
            wv3, wvr = wload('w_in', (l,), 0, D, voff, voff + 128)
            L = S_LEN // r

            def hview(c):
                return hT[:, c, :].rearrange("p (m r) -> p r m", r=r)

            def blk_cols(v, jb):
                if r == 1:
                    return v[:, 0, 128 * jb:128 * jb + 128]
                if r == 4:
                    return v[:, jb // 4, 128 * (jb % 4):128 * (jb % 4) + 128]
                return v[:, jb, :]

            def pdst(t2d, rows, i):
                if r == 1:
                    return t2d[rows, 512 * i:512 * i + 512]
                w = 512 // r
                return t2d[rows, :].rearrange("p (cc m) -> p cc m", cc=r)[:, :, w * i:w * i + w]

            def psrc(bank, rows):
                if r == 1:
                    return PS[bank][rows, :]
                return PS[bank][rows, :].rearrange("p (mm cc) -> p cc mm", cc=r)
            hr_all = [('hT', c, i) for c in range(NCH) for i in range(NTT)]
            allq = [('q', i) for i in range(NTT)]
            allq2 = [('q2', i) for i in range(NTT)]
            allk = [('k', i) for i in range(NTT)]
            lo, hi = slice(0, 64), slice(64, 128)
            al = slice(0, 128)
            for i in range(NTT):
                tsl = slice(512 * i, 512 * i + 512)
                bq = (2 * i) % 4
                bk = (2 * i + 1) % 4
                for c in range(NCH):
                    mm(PS[bq][:, :], wq3[:, c, :], hT[:, c, tsl], c == 0, c == NCH - 1, hres_all(i) + [wqr], [('ps', bq)])
                if dmode:
                    for qq, qn, mc in ((qT, 'q', 0), (q2T, 'q2', 2), (q3T, 'q3', 1), (q4T, 'q4', 3)):
                        ts('dve' , qq[:, tsl], PS[bq][:, :], cf[:, CF_M0 + mc:CF_M0 + mc + 1], None, ALU.mult, None,
                           [('ps', bq), 'cf'], [(qn, i)])
                else:
                    wq_ = [('q', i)] if r == 1 else allq
                    wq2_ = [('q2', i)] if r == 1 else allq2
                    act(pdst(qT, lo, i), psrc(bq, lo), AF.Copy, [('ps', bq)], wq_, scale=qscale)
                    ts('dve', pdst(q2T, hi, i), psrc(bq, hi), float(qscale), None, ALU.mult, None, [('ps', bq)], wq2_)
                for c in range(NCH):
                    mm(PS[bk][:, :], wk3[:, c, :], hT[:, c, tsl], c == 0, c == NCH - 1, hres_all(i) + [wkr], [('ps', bk)])
                cp('dve', pdst(kT, al, i), psrc(bk, al), [('ps', bk)], [('k', i)] if r == 1 else allk)
            for i in range(NTT):
                hr = hr_all if r > 1 else hres_all(i)
                bv = 4 + (i % 2)
                for jj in range(4):
                    jb = 4 * i + jj
                    for c in range(NCH):
                        mm(PS[bv][:, 128 * jj:128 * jj + 128], blk_cols(hview(c), jb), wv3[:, c, :], c == 0, c == NCH - 1,
                           hr + [wvr], [('ps', bv)])
                cp('act', vA[:, 4 * i:4 * i + 4, :, 0:64],
                   PS[bv][:, :].rearrange("p (j h e) -> p j h e", j=4, h=2),
                   [('ps', bv)], [('v', i)])

        def run_pipeline(tasks, stages, offs=None):
            n = len(tasks)
            ns = len(stages)
            if offs is None:
                offs = list(range(ns))
            for step in range(n + offs[-1]):
                for si, st in enumerate(stages):
                    k = step - offs[si]
                    if 0 <= k < n:
                        st(tasks[k])

        qres = lambda i: ('q', i)
        kres = lambda j: ('k', j // 4)
        vres = lambda j: ('v', j // 4)

        def zero_acc(bank, m=128):
            mm(PS[bank][0:m, :], cb[:, CB_ZERO:CB_ZERO + m], cb[:, 0:512], True, False, ['cb'], [('ps', bank)])

        def attn_dilated(g, hp, l):
            r = DIL_R[g]
            tasks = []
            grp = 0
            for hh in range(2):
                h = 2 * hp + hh
                coef = -float(SLOPES_DIL[g][h]) * r
                for i in range(NTT):
                    accb = 5 + (grp % 3)
                    grp += 1
                    tl = []
                    if r == 1:
                        if i > 0:
                            tl.append((4 * i - 1, 0, 128, 128))
                        for m in range(4):
                            tl.append((4 * i + m, 128 * m, min(128 * m + 256, 512), 0))
                    elif r == 4:
                        for m in range(4):
                            tl.append((4 * i + m, 128 * m, min(128 * m + 256, 512), 0))
                    else:
                        for m in range(4):
                            tl.append((4 * i + m, 128 * m, 128 * m + 128, 0))
                    for ti, (j, c0, c1, dwo) in enumerate(tl):
                        tasks.append(dict(hh=hh, h=h, i=i, j=j, c0=c0, c1=c1, dwo=dwo, coef=coef, accb=accb,
                                          first=(ti == 0), last=(ti == len(tl) - 1), idx=len(tasks)))

            def st_score(t):
                sbk = t['idx'] % 5
                t['sb'] = sbk
                rows = slice(64 * t['hh'], 64 * t['hh'] + 64)
                qq = qT if t['hh'] == 0 else q2T
                mm(PS[sbk][:, t['c0']:t['c1']], kT[:, 128 * t['j']:128 * t['j'] + 128], qq[:, 512 * t['i'] + t['c0']:512 * t['i'] + t['c1']],
                   True, True, [qres(t['i']), ('q2', t['i']), kres(t['j'])], [('ps', sbk)])

            def st_elem(t):
                sbk = t['sb']
                c0, c1 = t['c0'], t['c1']
                w = c1 - c0
                tmp, tr = nwf()
                stt('dve', tmp[:, c0:c1], cb[:, CB_DW + t['dwo']:CB_DW + t['dwo'] + w], t['coef'], PS[sbk][:, c0:c1], ALU.mult, ALU.add,
                    [('ps', sbk), 'cb'], [tr])
                pt, pr = nwh()
                act(pt[:, c0:c1], tmp[:, c0:c1], AF.Exp, [tr], [pr])
                t['pt'], t['pr'] = pt, pr

            def st_pv(t):
                c0, c1 = t['c0'], t['c1']
                for cc in range(c0, c1, 128):
                    mm(PS[t['accb']][:, cc:cc + 128], vA[:, t['j'], t['hh'], :], t['pt'][:, cc:cc + 128], (t['first'] and cc == c0),
                       (t['last'] and cc + 128 >= c1), [t['pr'], vres(t['j']), 'vones'], [('ps', t['accb'])], sgc=True)
                if t['last']:
                    i, hh = t['i'], t['hh']
                    av = accB[:, hh, :].rearrange("p (m r) -> p r m", r=r)
                    if r == 1:
                        dst = av[:, 0, 512 * i:512 * i + 512]
                        src = PS[t['accb']][:, :]
                    elif r == 4:
                        dst = av[:, i, :]
                        src = PS[t['accb']][:, :]
                    else:
                        dst = av[:, 4 * i:4 * i + 4, :]
                        src = PS[t['accb']][:, :].rearrange("p (a t) -> p a t", a=4)
                    if g == 0:
                        cp('act', dst, src, [('ps', t['accb'])], ares(hh))
                    else:
                        tmpc, tcr = nwf()
                        cp('act', tmpc[:, :], PS[t['accb']][:, :], [('ps', t['accb'])], [tcr])
                        srcs = tmpc[:, :] if r < 16 else tmpc[:, :].rearrange("p (a t) -> p a t", a=4)
                        tt('pool', dst, srcs, dst, ALU.add, [tcr] + ares(hh), ares(hh))
            run_pipeline(tasks, [st_score, st_elem, st_pv], [0, 2, 4])

        def finalize_dilated(hp):
            for hh in range(2):
                for i in range(NTT):
                    tsl = slice(512 * i, 512 * i + 512)
                    rc, rr = nwf()
                    recip(rc[0:64, :], accB[64:128, hh, tsl], ares(hh), [rr])
                    tt('dve', ysT[64 * hh:64 * hh + 64, 2 + hp, tsl], accB[0:64, hh, tsl], rc[0:64, :], ALU.mult,
                       ares(hh) + [rr], [('ys', 2 + hp, i)])

        def attn_stick(hp):
            tasks = []
            grp = 0
            Racc, Rr = WF[0], ('wf', 0)
            for hh in range(2):
                for i in range(NTT):
                    accb = 5 + (grp % 2)
                    grp += 1
                    js = list(range(4 * i + 3, -1, -1))
                    for ti, j in enumerate(js):
                        m = j - 4 * i
                        c0 = 128 * m if m > 0 else 0
                        tasks.append(dict(hh=hh, i=i, j=j, m=m, c0=c0, accb=accb, first=(ti == 0), last=(ti == len(js) - 1), idx=len(tasks)))
            rb = [None]

            def st_z(t):
                zb = t['idx'] % 3
                t['zb'] = zb
                rows = slice(64 * t['hh'], 64 * t['hh'] + 64)
                c0 = t['c0']
                qq = qT if t['hh'] == 0 else q2T
                mm(PS[zb][:, c0:512], kT[:, 128 * t['j']:128 * t['j'] + 128], qq[:, 512 * t['i'] + c0:512 * t['i'] + 512],
                   True, True, [qres(t['i']), ('q2', t['i']), kres(t['j'])], [('ps', zb)])

            def st_sp(t):
                c0 = t['c0']
                zb = t['zb']
                act(PS[zb][:, c0:512], PS[zb][:, c0:512], AF.Exp, [('ps', zb)], [('ps', zb)])
                sp, spr = nwh()
                act(sp[:, c0:512], PS[zb][:, c0:512], AF.Ln, [('ps', zb)], [spr], bias=1.0)
                if t['m'] >= 0:
                    tt('pool', sp[:, c0:c0 + 128], sp[:, c0:c0 + 128], cb[:, CB_TRIS:CB_TRIS + 128], ALU.mult, [spr, 'cb'], [spr])
                t['sp'], t['spr'] = sp, spr

            def st_l(t):
                c0 = t['c0']
                lb = 3 + (t['idx'] % 2)
                t['lb'] = lb
                rows = slice(64 * t['hh'], 64 * t['hh'] + 64)
                if t['first']:
                    memset('pool', Racc[:, :], 0.0, [Rr])
                qq = qT if t['hh'] == 0 else q2T
                mm(PS[lb][:, c0:512], kT[:, 128 * t['j']:128 * t['j'] + 128], qq[:, 512 * t['i'] + c0:512 * t['i'] + 512],
                   True, False, [qres(t['i']), ('q2', t['i']), kres(t['j'])], [('ps', lb)])
                mm(PS[lb][:, c0:512], cb[:, CB_NTRI:CB_NTRI + 128], t['sp'][:, c0:512], False, t['first'], [t['spr'], 'cb'], [('ps', lb)])
                if not t['first']:
                    rbt, rbr = rb[0]
                    mm(PS[lb][:, c0:512], cb[:, CB_NONES:CB_NONES + 128], rbt[:, c0:512], False, True, [rbr, 'cb'], [('ps', lb)])
                if not t['last']:
                    tt('dve', Racc[:, c0:512], Racc[:, c0:512], t['sp'][:, c0:512], ALU.add, [Rr, t['spr']], [Rr])
                    nb, nbr = nwh()
                    cp('dve', nb[:, :], Racc[:, :], [Rr], [nbr])
                    rb[0] = (nb, nbr)

            def st_a(t):
                c0 = t['c0']
                lb = t['lb']
                pt, pr = nwh()
                act(pt[:, c0:512], PS[lb][:, c0:512], AF.Exp, [('ps', lb)], [pr])
                if t['m'] >= 0:
                    tt('pool', pt[:, c0:c0 + 128], pt[:, c0:c0 + 128], cb[:, CB_TRIS:CB_TRIS + 128], ALU.mult, [pr, 'cb'], [pr])
                t['pt'], t['pr'] = pt, pr

            def st_pv(t):
                c0 = t['c0']
                mm(PS[t['accb']][:, c0:512], vA[:, t['j'], t['hh'], :], t['pt'][:, c0:512], t['first'], t['last'],
                   [t['pr'], vres(t['j']), 'vones'], [('ps', t['accb'])], sgc=True)
                if t['last']:
                    i, hh = t['i'], t['hh']
                    cp('dve', ysT[64 * hh:64 * hh + 64, hp, 512 * i:512 * i + 512], PS[t['accb']][0:64, :],
                       [('ps', t['accb'])], [('ys', hp, i)])

            def nwf_a():
                wfc[0] += 1
                i = 1 + (wfc[0] % (NWF - 1))
                return WF[i], ('wf', i)
            run_pipeline(tasks, [st_z, st_sp, st_l, st_a, st_pv])

        def fox_prep(l):
            wf3, wfr = wload('w_in', (l,), 0, D, OFF_F, OFF_F + 4)
            S.dma('sp', 'ld', lambda: nc.sync.dma_start(out=bfb[:, :], in_=bf_d[l:l + 1, :].broadcast_to([128, 4])), (), ['bfb'])
            for b in range(NKB):
                for c in range(NCH):
                    mm(PS[7][:, 4 * b:4 * b + 4], hT[:, c, 128 * b:128 * b + 128], wf3[:, c, 0:4], c == 0, c == NCH - 1,
                       hres_all(b // 4) + [wfr], [('ps', 7)])
            t0, t0r = nwf()
            tt('dve', t0[:, 0:64].rearrange("p (b h) -> p b h", h=4), PS[7][:, 0:64].rearrange("p (b h) -> p b h", h=4),
               bfb[:, :].unsqueeze(1).broadcast_to([128, NKB, 4]), ALU.add, [('ps', 7), 'bfb'], [t0r])
            t1, t1r = nwf()
            act(t1[:, 0:64], t0[:, 0:64], AF.Exp, [t0r], [t1r], scale=-1.0)
            t2, t2r = nwf()
            act(t2[:, 0:64], t1[:, 0:64], AF.Ln, [t1r], [t2r], bias=1.0)
            ts('dve', logf[:, :, :], t2[:, 0:64].rearrange("p (b h) -> p b h", h=4), -1.0, None, ALU.mult, None, [t2r], ['logf'])
            lf2 = logf[:, :, :].rearrange("p b h -> p (b h)")
            mm(PS[7][:, 64:128], cf[:, CF_TRI:CF_TRI + 128], lf2, True, True, ['logf', 'cf'], [('ps', 7)])
            mm(PS[7][:, 128:192], cf[:, CF_ONES:CF_ONES + 128], lf2, True, True, ['logf', 'cf'], [('ps', 7)])
            tb, tbr = nwf()
            tb3 = tb[:, 0:64].rearrange("p (b h) -> p b h", h=4)
            cp('dve', tb[:, 0:64], PS[7][:, 128:192], [('ps', 7)], [tbr])
            cp('dve', cend[:, 0, :], tb3[:, 0, :], [tbr], ['cend'])
            for j in range(1, NKB):
                tt('dve', cend[:, j, :], cend[:, j - 1, :], tb3[:, j, :], ALU.add, [tbr, 'cend'], ['cend'])
            cp('dve', cumtok[:, 0, :], PS[7][:, 64:68], [('ps', 7)], ['cumtok'])
            tt('dve', cumtok[:, 1:NKB, :], PS[7][:, 68:128].rearrange("p (b h) -> p b h", h=4), cend[:, 0:NKB - 1, :], ALU.add,
               [('ps', 7), 'cend'], ['cumtok'])
            for i in range(NTT):
                nj = 4 * i + 4
                for h in range(4):
                    if i == 0:
                        ts('dve', FB[:, i, 0:nj, h], cumtok[:, 0:nj, h], -1.0, None, ALU.mult, None, ['cumtok'], ['FB'])
                    else:
                        ts('dve', FB[:, i, 0:nj, h], cumtok[:, 0:nj, h], -1.0, cend[:, 4 * i - 1, h:h + 1], ALU.mult, ALU.add,
                           ['cumtok', 'cend'], ['FB'])

        def attn_fox(hp):
            tasks = []
            grp = 0
            CBs = [(WF[0], ('wf', 0)), (CB2, 'CB2')]
            groups = []

            def nwf_c():
                wfc[0] += 1
                i = 1 + (wfc[0] % (NWF - 1))
                return WF[i], ('wf', i)
            for hh in range(2):
                for i in range(NTT):
                    accb = 5 + (grp % 2)
                    gidx = grp
                    groups.append((2 * hp + hh, i))
                    grp += 1
                    js = list(range(4 * i, 4 * i + 4)) + list(range(0, 4 * i))
                    for ti, j in enumerate(js):
                        m = j - 4 * i
                        tasks.append(dict(hh=hh, h=2 * hp + hh, g=gidx, i=i, j=j, m=m, c0=(128 * m if m > 0 else 0), accb=accb,
                                          first=(ti == 0), last=(ti == len(js) - 1), idx=len(tasks)))

            def prep_group(g):
                h, i = groups[g]
                CBt, CBr = CBs[g % 2]
                for bb in range(4):
                    b = 4 * i + bb
                    mm(PS[7][0:1, 128 * bb:128 * bb + 128], logf[:, b, h:h + 1], cf[:, CF_TRI:CF_TRI + 128], bb == 0, bb == 3,
                       ['logf', 'cf'], [('ps', 7)])
                    if bb < 3:
                        mm(PS[7][0:1, 128 * bb + 128:512], logf[:, b, h:h + 1], cf[:, CF_ONES:CF_ONES + (384 - 128 * bb)], False, False,
                           ['logf', 'cf'], [('ps', 7)])
                cp('act', clr[0:1, :], PS[7][0:1, :], [('ps', 7)], ['clr'])
                mm(PS[7][:, :], cf[0:1, CF_ONES:CF_ONES + 128], clr[0:1, :], True, True, ['clr', 'cf'], [('ps', 7)])
                cp('act', CBt[:, :], PS[7][:, :], [('ps', 7)], [CBr])

            def st_score(t):
                sbk = t['idx'] % 5
                t['sb'] = sbk
                rows = slice(64 * t['hh'], 64 * t['hh'] + 64)
                c0 = t['c0']
                qq = qT if t['hh'] == 0 else q2T
                mm(PS[sbk][:, c0:512], kT[:, 128 * t['j']:128 * t['j'] + 128], qq[:, 512 * t['i'] + c0:512 * t['i'] + 512],
                   True, True, [qres(t['i']), ('q2', t['i']), kres(t['j'])], [('ps', sbk)])

            def st_elem(t):
                sbk = t['sb']
                c0 = t['c0']
                if t['first']:
                    if t['g'] == 0:
                        prep_group(0)
                    if t['g'] + 1 < len(groups):
                        prep_group(t['g'] + 1)
                CBt, CBr = CBs[t['g'] % 2]
                tmp, tr = nwf_c()
                fbc = FB[:, t['i'], t['j'], t['h']:t['h'] + 1]
                stt('dve', tmp[:, c0:512], PS[sbk][:, c0:512], fbc, CBt[:, c0:512], ALU.add, ALU.add, [('ps', sbk), 'FB', CBr], [tr])
                if t['m'] >= 0:
                    tt('pool', tmp[:, c0:c0 + 128], tmp[:, c0:c0 + 128], cf[:, CF_NEGM:CF_NEGM + 128], ALU.add, [tr, 'cf'], [tr])
                pt, pr = nwh()
                act(pt[:, c0:512], tmp[:, c0:512], AF.Exp, [tr], [pr])
                t['pt'], t['pr'] = pt, pr

            def st_pv(t):
                c0 = t['c0']
                mm(PS[t['accb']][:, c0:512], vA[:, t['j'], t['hh'], :], t['pt'][:, c0:512], t['first'], t['last'],
                   [t['pr'], vres(t['j']), 'vones'], [('ps', t['accb'])])
                if t['last']:
                    i, hh = t['i'], t['hh']
                    rc, rr = nwf_c()
                    recip(rc[0:64, :], PS[t['accb']][64:128, :], [('ps', t['accb'])], [rr])
                    tt('dve', ysT[64 * hh:64 * hh + 64, 4 + hp, 512 * i:512 * i + 512], PS[t['accb']][0:64, :], rc[0:64, :], ALU.mult,
                       [('ps', t['accb']), rr], [('ys', 4 + hp, i)])
            run_pipeline(tasks, [st_score, st_elem, st_pv], [0, 2, 4])

        def diff_prep(l, lam_init):
            for k, src in enumerate([lq1_d, lk1_d, lq2_d, lk2_d]):
                S.dma('sp', 'ld', (lambda k=k, src=src: nc.sync.dma_start(out=ltab[:, k, :], in_=src[l:l + 1, :].broadcast_to([128, 32]))), (), ['ltab'])
            S.dma('sp', 'ld', lambda: nc.sync.dma_start(out=lsm[0:64, 8:9], in_=gd_d[l:l + 1, :].rearrange("o d -> d o"), allow_slow_non_contiguous=True), (), ['lsm8'])
            S.dma('sp', 'ld', lambda: nc.sync.dma_start(out=lsm[64:128, 8:9], in_=gd_d[l:l + 1, :].rearrange("o d -> d o"), allow_slow_non_contiguous=True), (), ['lsm8'])
            pr1, pr1r = nwf()
            tt('dve', pr1[:, 0:32], ltab[:, 0, :], ltab[:, 1, :], ALU.mult, ['ltab'], [pr1r])
            S.op('dve', lambda: nc.vector.reduce_sum(out=lsm[:, 0:1], in_=pr1[:, 0:32], axis=AX.X), [pr1r], ['lsm0'])
            pr2, pr2r = nwf()
            tt('dve', pr2[:, 0:32], ltab[:, 2, :], ltab[:, 3, :], ALU.mult, ['ltab'], [pr2r])
            S.op('dve', lambda: nc.vector.reduce_sum(out=lsm[:, 1:2], in_=pr2[:, 0:32], axis=AX.X), [pr2r], ['lsm0'])
            act(lsm[:, 2:4], lsm[:, 0:2], AF.Exp, ['lsm0'], ['lsm2'])
            tt('dve', lsm[:, 4:5], lsm[:, 3:4], lsm[:, 2:3], ALU.subtract, ['lsm2'], ['lsm4'])
            ts('dve', lsm[:, 5:6], lsm[:, 4:5], -float(lam_init), None, ALU.add, None, ['lsm4'], ['neglam'])
            ts('dve', lsm[:, 9:10], lsm[:, 8:9], float(1.0 - lam_init), None, ALU.mult, None, ['lsm8'], ['gdcol'])

        def attn_diff(hp):
            tasks = []
            Yt, Yr = WF[0], ('wf', 0)

            def nwf_d():
                wfc[0] += 1
                i = 1 + (wfc[0] % (NWF - 1))
                return WF[i], ('wf', i)
            grp = 0
            for i in reversed(range(NTT)):
                for hh in range(2):
                    for cm in range(2):
                        accb = 3 + (grp % 4)
                        grp += 1
                        js = list(range(4 * i, 4 * i + 4)) + list(range(0, 4 * i))
                        for ti, j in enumerate(js):
                            m = j - 4 * i
                            tasks.append(dict(hh=hh, h=2 * hp + hh, cm=cm, i=i, j=j, m=m, c0=(128 * m if m > 0 else 0), accb=accb,
                                              first=(ti == 0), last=(ti == len(js) - 1), idx=len(tasks)))
            accs = {}

            def st_score(t):
                sbk = t['idx'] % 3
                t['sb'] = sbk
                rows = slice(64 * t['hh'], 64 * t['hh'] + 64)
                c0 = t['c0']
                qq = ((qT, q3T), (q2T, q4T))[t['hh']][t['cm']]
                mm(PS[sbk][:, c0:512], kT[:, 128 * t['j']:128 * t['j'] + 128], qq[:, 512 * t['i'] + c0:512 * t['i'] + 512],
                   True, True, [qres(t['i']), ('q2', t['i']), ('q3', t['i']), ('q4', t['i']), kres(t['j'])], [('ps', sbk)])

            def st_elem(t):
                sbk = t['sb']
                c0 = t['c0']
                pt, pr = nwh()
                mmi = t['j'] - 4 * t['i'] + 12
                col = CF_DB + 16 * t['h'] + mmi
                act(pt[:, c0:512], PS[sbk][:, c0:512], AF.Exp, [('ps', sbk), 'cf'], [pr], bias=cf[:, col:col + 1], scale=float(32 ** -0.5))
                if t['m'] >= 0:
                    tt('pool', pt[:, c0:c0 + 128], pt[:, c0:c0 + 128], cb[:, CB_TRII:CB_TRII + 128], ALU.mult, [pr, 'cb'], [pr])
                t['pt'], t['pr'] = pt, pr

            pending = []

            def st_pv(t):
                c0 = t['c0']
                while pending and pending[0][0] <= t['idx']:
                    pending.pop(0)[1]()
                mm(PS[t['accb']][:, c0:512], vA[:, t['j'], t['hh'], :], t['pt'][:, c0:512], t['first'], t['last'],
                   [t['pr'], vres(t['j']), 'vones'], [('ps', t['accb'])])
                if t['last']:
                    accs[(t['i'], t['hh'], t['cm'])] = t['accb']
                    if t['cm'] == 1:
                        i, hh = t['i'], t['hh']
                        a0 = accs[(i, hh, 0)]
                        a1 = accs[(i, hh, 1)]
                        r0, r0r = nwf_d()
                        recip_dve(r0[0:64, :], PS[a0][64:128, :], [('ps', a0)], [r0r])
                        r1, r1r = nwf_d()
                        recip_dve(r1[0:64, :], PS[a1][64:128, :], [('ps', a1)], [r1r])
                        y0, y0r = nwf_d()
                        tt('dve', y0[0:64, :], PS[a0][0:64, :], r0[0:64, :], ALU.mult, [('ps', a0), r0r], [y0r])
                        tt('dve', r1[0:64, :], PS[a1][0:64, :], r1[0:64, :], ALU.mult, [('ps', a1), r1r], [r1r])
                        stt('dve', Yt[64 * hh:64 * hh + 64, :], r1[0:64, :], lsm[0:64, 5:6], y0[0:64, :], ALU.mult, ALU.add,
                            [r1r, y0r, 'neglam'], [Yr])
                        if hh == 1:
                            def f2(i=i):
                                sq, sqr = nwh()
                                act(sq[:, :], Yt[:, :], AF.Square, [Yr], [sqr])
                                mm(PS[7][:, :], cb[:, CB_BD:CB_BD + 128], sq[:, :], True, True, [sqr, 'cb'], [('ps', 7)])
                                lnv, lnr = nwf_d()
                                act(lnv[:, :], PS[7][:, :], AF.Ln, [('ps', 7)], [lnr], bias=EPS, scale=1.0 / 64)
                                rs, rsr = nwf_d()
                                act(rs[:, :], lnv[:, :], AF.Exp, [lnr], [rsr], scale=-0.5)
                                stt('dve', ysT[:, 6 + hp, 512 * i:512 * i + 512], Yt[:, :], lsm[:, 9:10], rs[:, :], ALU.mult, ALU.mult,
                                    [Yr, rsr, 'gdcol'], [('ys', 6 + hp, i)])
                            pending.append((t['idx'] + 6, f2))
            run_pipeline(tasks, [st_score, st_elem, st_pv])

            def flush():
                while pending:
                    pending.pop(0)[1]()
            return flush

        def gate_and_out(l):
            for hf in range(2):
                for dcc in range(4):
                    dc = 4 * hf + dcc
                    maccs = [None] * NTT
                    for n in range(4):
                        wg3, wgr = wload('w_in', (l,), 0, D, OFF_G + n * D + 128 * dc, OFF_G + n * D + 128 * dc + 128)
                        wb3, wbr = wload('w_branch', (l, n), 0, 256, 128 * dc, 128 * dc + 128)
                        for i in range(NTT):
                            tsl = slice(512 * i, 512 * i + 512)
                            gb = (2 * i) % 6
                            bb = (2 * i + 1) % 6
                            for c in range(NCH):
                                mm(PS[gb][:, :], wg3[:, c, :], hT[:, c, tsl], c == 0, c == NCH - 1, hres_all(i) + [wgr], [('ps', gb)])
                            sg, sgr = nwf2()
                            act(sg[:, :], PS[gb][:, :], AF.Sigmoid, [('ps', gb)], [sgr])
                            for cc in range(2):
                                mm(PS[bb][:, :], wb3[:, cc, :], ysT[:, 2 * n + cc, tsl], cc == 0, cc == 1,
                                   [('ys', 2 * n + cc, i), wbr], [('ps', bb)])
                            if n == 0:
                                maccs[i] = (WF[i], ('wf', i))
                                tt('dve', WF[i][:, :], sg[:, :], PS[bb][:, :], ALU.mult, [sgr, ('ps', bb)], [('wf', i)])
                            else:
                                tt('dve', sg[:, :], sg[:, :], PS[bb][:, :], ALU.mult, [sgr, ('ps', bb)], [sgr])
                                if n < 3:
                                    tt('pool', WF[i][:, :], WF[i][:, :], sg[:, :], ALU.add, [('wf', i), sgr], [('wf', i)])
                                else:
                                    tt('pool', MB[:, dcc, tsl], WF[i][:, :], sg[:, :], ALU.add, [('wf', i), sgr], mbres(dcc, i))
                for dco in range(NCH):
                    wo3, wor = wload('w_out', (l,), 512 * hf, 512 * hf + 512, 128 * dco, 128 * dco + 128)
                    for i in range(NTT):
                        tsl = slice(512 * i, 512 * i + 512)
                        ob = (dco * NTT + i) % 6
                        for dcc in range(4):
                            mm(PS[ob][:, :], wo3[:, dcc, :], MB[:, dcc, tsl], dcc == 0, dcc == 3, mbres(dcc, i) + [wor], [('ps', ob)])
                        tt('dve', xT[:, dco, tsl], PS[ob][:, :], xT[:, dco, tsl], ALU.add, [('ps', ob), xres(dco, i)], [xres(dco, i)])

        def nwf2():
            wfc[0] += 1
            i = 4 + (wfc[0] % 2)
            return WF[i], ('wf', i)

        def mlp(l):
            upT = ysT
            for f in range(4):
                for fc in range(NCH):
                    col = 1024 * f + 128 * fc
                    wu3, wur = wload('w_up', (l,), 0, D, col, col + 128)
                    for i in range(NTT):
                        tsl = slice(512 * i, 512 * i + 512)
                        ub = (fc * NTT + i) % 6
                        for c in range(NCH):
                            mm(PS[ub][:, :], wu3[:, c, :], hT[:, c, tsl], c == 0, c == NCH - 1, hres_all(i) + [wur], [('ps', ub)])
                        rl, rlr = nwf()
                        act(rl[:, :], PS[ub][:, :], AF.Relu, [('ps', ub)], [rlr])
                        act(upT[:, fc, tsl], rl[:, :], AF.Square, [rlr], [('ys', fc, i)])
                for dco in range(NCH):
                    wd3, wdr = wload('w_down', (l,), 1024 * f, 1024 * f + 1024, 128 * dco, 128 * dco + 128)
                    for i in range(NTT):
                        tsl = slice(512 * i, 512 * i + 512)
                        db = (dco * NTT + i) % 6
                        for fc in range(NCH):
                            mm(PS[db][:, :], wd3[:, fc, :], upT[:, fc, tsl], fc == 0, fc == NCH - 1, [('ys', fc, i), wdr], [('ps', db)])
                        tt('dve', xT[:, dco, tsl], PS[db][:, :], xT[:, dco, tsl], ALU.add, [('ps', db), xres(dco, i)], [xres(dco, i)])

        try:
            chk('load')
            for l in layers:
                lam_init = 0.8 - 0.6 * math.exp(-0.3 * l)
                rmsnorm_to_hT(2 * l)
                chk('norm')
                memset('pool', vA[:, :, :, 64:128], 1.0, ['vones'])
                memset('pool', qT[64:128, :], 0.0, [('q', i) for i in range(NTT)])
                memset('pool', q2T[0:64, :], 0.0, [('q2', i) for i in range(NTT)])
                for hp in range(2):
                    for g in range(3):
                        project_unit(l, OFF_QB + g * 256 + hp * 128, OFF_KB + g * 256 + hp * 128, OFF_VB + g * 256 + hp * 128, DIL_R[g], 0.125)
                        chk('projB%d' % g)
                        attn_dilated(g, hp, l)
                        chk('attnB%d' % g)
                    finalize_dilated(hp)
                    chk('finB')
                for hp in range(2):
                    project_unit(l, OFF_QA + hp * 128, OFF_KA + hp * 128, OFF_VA + hp * 128, 1, 0.125)
                    attn_stick(hp)
                    chk('A')
                fox_prep(l)
                chk('foxprep')
                for hp in range(2):
                    project_unit(l, OFF_QC + hp * 128, OFF_KC + hp * 128, OFF_VC + hp * 128, 1, 0.125)
                    attn_fox(hp)
                    chk('C')
                diff_prep(l, lam_init)
                chk('diffprep')
                carry = None
                for hp in range(2):
                    project_unit(l, OFF_QD + hp * 128, OFF_KD + hp * 128, OFF_VD + hp * 128, 1, 1.0, dmode=True)
                    if carry is not None:
                        carry()
                    chk('projD')
                    carry = attn_diff(hp)
                    chk('D')
                carry()
                if dbg == 'ys':
                    break
                gate_and_out(l)
                chk('gate')
                rmsnorm_to_hT(2 * l + 1)
                mlp(l)
                chk('mlp')
        except _Stop:
            pass

        if dbg == 'ys':
            for c in range(NCH):
                S.dma('pool', 'dbg', (lambda c=c: nc.gpsimd.dma_start(out=dbg_d[:, S_LEN * c:S_LEN * c + S_LEN], in_=ysT[:, c, :])),
                      [('ys', c, i) for i in range(NTT)], [])

        if final_norm:
            for i in range(NTT):
                tsl = slice(512 * i, 512 * i + 512)
                for c in range(NCH):
                    sq, sqr = nwh()
                    act(sq[:, :], xT[:, c, tsl], AF.Square, [xres(c, i)], [sqr])
                    mm(PS[7][:, :], cb[:, CB_ONES:CB_ONES + 128], sq[:, :], c == 0, c == NCH - 1, [sqr, 'cb'], [('ps', 7)])
                lnv, lnr = nwf()
                act(lnv[:, :], PS[7][:, :], AF.Ln, [('ps', 7)], [lnr], bias=EPS, scale=1.0 / D)
                rs, rsr = nwf()
                act(rs[:, :], lnv[:, :], AF.Exp, [lnr], [rsr], scale=-0.5)
                for c in range(NCH):
                    eng = 'dve'
                    stt(eng, xT[:, c, tsl], xT[:, c, tsl], gT[:, 4, c:c + 1], rs[:, :], ALU.mult, ALU.mult,
                        [xres(c, i), rsr, 'gT'], [xres(c, i)])
        for b in range(NKB):
            sl = b % 2
            for half in range(2):
                bank = (2 * b + half) % 4
                for cc in range(4):
                    c = 4 * half + cc
                    S.op('pe', (lambda bank=bank, cc=cc, c=c, b=b: nc.tensor.transpose(
                        out=PS[bank][:, 128 * cc:128 * cc + 128], in_=xT[:, c, 128 * b:128 * b + 128], identity=ident[:, :])),
                        [xres(c, b // 4), 'ident'], [('ps', bank)])
                eng = 'dve' if half == 0 else 'act'
                cp(eng, stg[:, sl, 512 * half:512 * half + 512], PS[bank][:, :], [('ps', bank)], stgres(sl))
            S.dma('sp', 'st%d' % sl, (lambda b=b, sl=sl: nc.sync.dma_start(out=y_d[128 * b:128 * b + 128, :], in_=stg[:, sl, :])),
                  stgres(sl), [])

        with nc.Block() as block:
            def emit(engname, e):
                for (fn, waits, sk, amt) in S.streams[engname]:
                    for (k, v) in waits:
                        e.wait_ge(sems[k], v)
                    ins = fn()
                    ins.then_inc(sems[sk], amt)

            @block.tensor
            def _(e):
                emit('pe', e)

            @block.scalar
            def _(e):
                emit('act', e)

            @block.vector
            def _(e):
                emit('dve', e)

            @block.gpsimd
            def _(e):
                emit('pool', e)
                if 'dbg' in S.dcnt:
                    e.wait_ge(sems['dbg'], S.dcnt['dbg'])

            @block.sync
            def _(e):
                emit('sp', e)
                e.wait_ge(sems['st0'], S.dcnt['st0'])
                e.wait_ge(sems['st1'], S.dcnt['st1'])
    nc._wspecs = recorded
    return nc


_CONSTS = None


def _run(layers, final_norm, xin, inputs, dbg=None, stop_stage=None):
    global _CONSTS
    if _CONSTS is None:
        _CONSTS = _host_consts()
    cfh, cbh, identh = _CONSTS
    nc0 = build_program(layers, final_norm, dbg=dbg, stop_stage=stop_stage)
    nc = build_program(layers, final_norm, dbg=dbg, stop_stage=stop_stage, wspecs=list(nc0._wspecs))
    f = lambda a: np.ascontiguousarray(np.asarray(a, dtype=np.float32))
    shared = {
        "w_in": f(inputs["w_in"]), "w_branch": f(inputs["w_branch"]), "w_out": f(inputs["w_out"]),
        "w_up": f(inputs["w_up"]), "w_down": f(inputs["w_down"]),
        "mix_norm_g": f(inputs["mix_norm_g"]), "mlp_norm_g": f(inputs["mlp_norm_g"]),
        "final_norm_g": f(inputs["final_norm_g"]).reshape(1, D),
        "b_forget": f(inputs["b_forget"]),
        "lambda_q1": f(inputs["lambda_q1"]), "lambda_k1": f(inputs["lambda_k1"]),
        "lambda_q2": f(inputs["lambda_q2"]), "lambda_k2": f(inputs["lambda_k2"]),
        "diff_norm_g": f(inputs["diff_norm_g"]),
        "cf": cfh, "cb": cbh, "ident": identh,
    }
    n = xin.shape[0]
    in_maps = []
    for b in range(n):
        m = dict(shared)
        m["x"] = np.ascontiguousarray(xin[b])
        in_maps.append(m)
    res = run_bass_kernel_spmd(nc, in_maps, core_ids=list(range(n)))
    return res


def kernel(**inputs):
    x = np.asarray(inputs["x"], dtype=np.float32)
    res = _run([0, 1], True, x, inputs)
    return np.stack([r["y"] for r in res.results], axis=0).astype(np.float32)
```

```python
import math
from contextlib import ExitStack
import numpy as np
import concourse.bass as bass
import concourse.mybir as mybir
from concourse.bass_utils import run_bass_kernel_spmd

F32 = mybir.dt.float32
BF16 = mybir.dt.bfloat16
AF = mybir.ActivationFunctionType
ALU = mybir.AluOpType
AX = mybir.AxisListType

S_LEN = 2048
D = 1024
NCH = 8
NTT = 4
NKB = 16
D_IN = 8708
DEPTH = 2
EPS = 1e-6
BIG = 30000.0
NWB = 6

OFF_QA, OFF_KA, OFF_VA = 0, 256, 512
OFF_QB, OFF_KB, OFF_VB = 768, 768 + 768, 768 + 1536
OFF_QC, OFF_KC, OFF_VC = 3072, 3328, 3584
OFF_F = 3840
OFF_QD, OFF_KD, OFF_VD = 3844, 4100, 4356
OFF_G = 4612

_SL = 2.0 ** (-8.0 * np.arange(1, 17) / 16.0)
SLOPES_DIL = np.stack([_SL[0:4], _SL[8:12], _SL[12:16]], 0)
SLOPES_DIFF = _SL[4:8]
DIL_R = (1, 4, 16)

CF_ONES = 0
CF_TRI = 384
CF_SEL127 = 512
CF_NEGM = 640
CF_DB = 768
CF_M0 = 768 + 64
CF_N = 768 + 68
CB_ONES = 0
CB_NTRI = 128
CB_NONES = 256
CB_BD = 384
CB_ZERO = 512
CB_TRII = 640
CB_TRIS = 768
CB_DW = 896
CB_N = 896 + 256


def _host_consts():
    p = np.arange(128)[:, None].astype(np.float64)
    c = np.arange(128)[None, :].astype(np.float64)
    cf = np.zeros((128, CF_N), np.float32)
    cf[:, CF_ONES:CF_ONES + 384] = 1.0
    cf[:, CF_TRI:CF_TRI + 128] = (p <= c)
    cf[:, CF_SEL127:CF_SEL127 + 128] = (p == 127)
    cf[:, CF_NEGM:CF_NEGM + 128] = np.where(c >= p, 0.0, -BIG)
    for h in range(4):
        for mm in range(16):
            cf[:, CF_DB + h * 16 + mm] = SLOPES_DIFF[h] * (p[:, 0] + 128.0 * (mm - 12) - 256.0)
    for k in range(4):
        cf[:, CF_M0 + k] = ((p[:, 0] // 32) == k)
    cb = np.zeros((128, CB_N), np.float32)
    cb[:, CB_ONES:CB_ONES + 128] = 1.0
    cb[:, CB_NTRI:CB_NTRI + 128] = np.where(p >= c, -1.0, 0.0)
    cb[:, CB_NONES:CB_NONES + 128] = -1.0
    cb[:, CB_BD:CB_BD + 128] = ((p // 64) == (c // 64))
    cb[:, CB_TRII:CB_TRII + 128] = (c >= p)
    cb[:, CB_TRIS:CB_TRIS + 128] = (c > p)
    x = np.arange(256)[None, :].astype(np.float64)
    dd = x - p
    cb[:, CB_DW:CB_DW + 256] = np.where((dd >= 0) & (dd <= 128), dd, BIG)
    ident = np.eye(128, dtype=np.float32)
    return cf, cb, ident


class Sched:
    CE = ('pe', 'act', 'dve', 'pool')

    def __init__(self):
        self.streams = {e: [] for e in ('pe', 'act', 'dve', 'pool', 'sp')}
        self.cnt = {e: 0 for e in self.CE}
        self.dcnt = {}
        self.lastw = {}
        self.readers = {}
        self.seen = {e: {} for e in self.streams}

    def _waits(self, eng, reads, writes):
        best = {}

        def add(tok):
            k, v = tok
            if k == 'pe' and eng == 'pe':
                return
            if best.get(k, 0) < v:
                best[k] = v
        for r in reads:
            t = self.lastw.get(r)
            if t:
                add(t)
            if isinstance(r, tuple) and r[0] == 'ps':
                for k, v in self.readers.get(r, {}).items():
                    if k != eng:
                        add((k, v))
        for w in writes:
            t = self.lastw.get(w)
            if t:
                add(t)
            for k, v in self.readers.get(w, {}).items():
                add((k, v))
        out = []
        for k, v in best.items():
            if self.seen[eng].get(k, 0) >= v:
                continue
            self.seen[eng][k] = v
            out.append((k, v))
        return out

    def _reg(self, tok, reads, writes):
        k, v = tok
        for r in reads:
            d = self.readers.setdefault(r, {})
            if d.get(k, 0) < v:
                d[k] = v
        for w in writes:
            self.lastw[w] = tok
            self.readers[w] = {}

    def op(self, eng, fn, reads=(), writes=()):
        waits = self._waits(eng, reads, writes)
        self.cnt[eng] += 1
        tok = (eng, self.cnt[eng])
        self.streams[eng].append((fn, waits, eng, 1))
        self._reg(tok, reads, writes)

    def dma(self, q, sem, fn, reads=(), writes=()):
        waits = self._waits(q, reads, writes)
        prev = self.dcnt.get(sem, 0)
        if prev > 0 and self.seen[q].get(sem, 0) < prev:
            self.seen[q][sem] = prev
            waits.append((sem, prev))
        self.dcnt[sem] = prev + 16
        tok = (sem, self.dcnt[sem])
        self.streams[q].append((fn, waits, sem, 16))
        self._reg(tok, reads, writes)


MMLOG = []


class _Stop(Exception):
    pass


def build_program(layers, final_norm, dbg=None, stop_stage=None, wspecs=None):
    del MMLOG[:]
    stage = [0]

    def chk(name):
        stage[0] += 1
        if stop_stage is not None and stage[0] >= stop_stage:
            print('STOP at stage', stage[0], name)
            raise _Stop()
    nc = bass.Bass("TRN2", target_bir_lowering=False, dynamic_dma_scratch_size=8192)
    S = Sched()

    def din(name, shape):
        return nc.dram_tensor(name, list(shape), F32, kind="ExternalInput").ap()
    x_d = din("x", [S_LEN, D])
    w_in_d = din("w_in", [DEPTH, D, D_IN])
    w_br_d = din("w_branch", [DEPTH, 4, 256, D])
    w_out_d = din("w_out", [DEPTH, D, D])
    w_up_d = din("w_up", [DEPTH, D, 4 * D])
    w_dn_d = din("w_down", [DEPTH, 4 * D, D])
    g_mix_d = din("mix_norm_g", [DEPTH, D])
    g_mlp_d = din("mlp_norm_g", [DEPTH, D])
    g_fin_d = din("final_norm_g", [1, D])
    bf_d = din("b_forget", [DEPTH, 4])
    lq1_d = din("lambda_q1", [DEPTH, 32])
    lk1_d = din("lambda_k1", [DEPTH, 32])
    lq2_d = din("lambda_q2", [DEPTH, 32])
    lk2_d = din("lambda_k2", [DEPTH, 32])
    gd_d = din("diff_norm_g", [DEPTH, 64])
    cf_d = din("cf", [128, CF_N])
    cb_d = din("cb", [128, CB_N])
    id_d = din("ident", [128, 128])
    y_d = nc.dram_tensor("y", [S_LEN, D], F32, kind="ExternalOutput").ap()
    dbg_d = None
    if dbg:
        dbg_d = nc.dram_tensor("dbg", [128, 8 * S_LEN], F32, kind="ExternalOutput").ap()

    es = ExitStack()
    with es:
        def sb(name, shape, dt):
            return es.enter_context(nc.sbuf_tensor(name, list(shape), dt))
        xT = sb("xT", [128, NCH, S_LEN], F32)
        hT = sb("hT", [128, NCH, S_LEN], BF16)
        ARA = sb("ARA", [128, 8192], F32)
        ysT = ARA[:, :].bitcast(BF16).rearrange("p (c t) -> p c t", c=NCH)
        accB = ARA[:, 4096:8192].rearrange("p (h t) -> p h t", h=2)
        stg = ARA[:, 0:2048].rearrange("p (b f) -> p b f", b=2)
        ARU = sb("ARU", [128, 8192], BF16)
        qT = ARU[:, 0:2048]
        kT = ARU[:, 2048:4096]
        vA = ARU[:, 4096:8192].rearrange("p (j h e) -> p j h e", j=NKB, h=2)
        MB = ARU[:, :].rearrange("p (c t) -> p c t", c=4)
        q2T = sb("q2T", [128, S_LEN], BF16)
        q3T = sb("q3T", [128, S_LEN], BF16)
        q4T = sb("q4T", [128, S_LEN], BF16)
        wbuf = [sb("wb%d" % i, [128, 1024], BF16) for i in range(NWB)]
        NWF, NWH = 6, 6
        WF = [sb("wf%d" % i, [128, 512], F32) for i in range(NWF)]
        WH = [sb("wh%d" % i, [128, 512], BF16) for i in range(NWH)]
        cf = sb("cf_s", [128, CF_N], F32)
        cb = sb("cb_s", [128, CB_N], BF16)
        ident = sb("ident_s", [128, 128], F32)
        gT = sb("gT", [128, 5, NCH], F32)
        ltab = sb("ltab", [128, 4, 32], F32)
        lsm = sb("lsm", [128, 16], F32)
        bfb = sb("bfb", [128, 4], F32)
        logf = sb("logf", [128, NKB, 4], F32)
        cumtok = sb("cumtok", [128, NKB, 4], F32)
        cend = sb("cend", [128, NKB, 4], F32)
        FB = sb("FB", [128, NTT, NKB, 4], F32)
        CB2 = sb("CB2", [128, 512], F32)
        clr = sb("clr", [1, 512], F32)
        PS = [es.enter_context(nc.psum_tensor("ps%d" % i, [128, 512], F32)) for i in range(8)]

        sem_names = ['pe', 'act', 'dve', 'pool'] + ['w%d' % i for i in range(NWB)] + ['ld', 'ldg', 'ld0', 'ld1', 'st0', 'st1', 'dbg']
        sems = {n: es.enter_context(nc.semaphore(n)) for n in sem_names}

        ENG = {'pe': nc.tensor, 'act': nc.scalar, 'dve': nc.vector, 'pool': nc.gpsimd, 'sp': nc.sync}
        wfc = [0]
        whc = [0]

        def nwf():
            wfc[0] += 1
            i = wfc[0] % NWF
            return WF[i], ('wf', i)

        def nwh():
            whc[0] += 1
            i = whc[0] % NWH
            return WH[i], ('wh', i)

        def mm(out, lhsT, rhs, start, stop, reads, writes, sgc=False):
            if sgc:
                S.op('pe', lambda: nc.tensor.matmul(out, lhsT=lhsT, rhs=rhs, start=start, stop=stop, skip_group_check=True), reads, writes)
            else:
                S.op('pe', lambda: nc.tensor.matmul(out, lhsT=lhsT, rhs=rhs, start=start, stop=stop), reads, writes)

        def act(out, in_, func, reads, writes, bias=None, scale=None):
            kw = {}
            if bias is not None:
                kw['bias'] = bias
            if scale is not None:
                kw['scale'] = scale
            S.op('act', lambda: nc.scalar.activation(out=out, in_=in_, func=func, **kw), reads, writes)

        def tt(eng, out, in0, in1, op, reads, writes):
            S.op(eng, lambda: ENG[eng].tensor_tensor(out=out, in0=in0, in1=in1, op=op), reads, writes)

        def stt(eng, out, in0, scalar, in1, op0, op1, reads, writes):
            S.op(eng, lambda: ENG[eng].scalar_tensor_tensor(out=out, in0=in0, scalar=scalar, in1=in1, op0=op0, op1=op1), reads, writes)

        def ts(eng, out, in0, s1, s2, op0, op1, reads, writes):
            if s2 is None:
                S.op(eng, lambda: ENG[eng].tensor_scalar(out=out, in0=in0, scalar1=s1, scalar2=None, op0=op0), reads, writes)
            else:
                S.op(eng, lambda: ENG[eng].tensor_scalar(out=out, in0=in0, scalar1=s1, scalar2=s2, op0=op0, op1=op1), reads, writes)

        def cp(eng, out, in_, reads, writes):
            if eng == 'act':
                S.op('act', lambda: nc.scalar.copy(out=out, in_=in_), reads, writes)
            else:
                S.op(eng, lambda: ENG[eng].tensor_copy(out=out, in_=in_), reads, writes)

        def recip_dve(out, in_, reads, writes):
            S.op('dve', lambda: nc.vector.reciprocal(out=out, in_=in_), reads, writes)

        def recip(out, in_, reads, writes):
            act(out, in_, AF.Ln, reads, writes)
            act(out, out, AF.Exp, writes, writes, scale=-1.0)

        def memset(eng, ap, val, writes):
            S.op(eng, lambda: ENG[eng].memset(ap, val), (), writes)

        DR = {'w_in': w_in_d, 'w_branch': w_br_d, 'w_out': w_out_d, 'w_up': w_up_d, 'w_down': w_dn_d}
        wcount = [0]
        wissued = [0]
        recorded = []
        WPF = 3

        def _wissue(k, spec):
            tname, idx, r0, r1, c0, c1 = spec
            kc = (r1 - r0) // 128
            ncols = c1 - c0
            i = k % NWB
            dst = wbuf[i][:, 0:kc * 128].rearrange("p (c n) -> p c n", c=kc)[:, :, 0:ncols]
            src = DR[tname][tuple(idx) + (slice(r0, r1), slice(c0, c1))].rearrange("(c p) n -> p c n", p=128)
            S.dma('pool', 'w%d' % i, lambda: nc.gpsimd.dma_start(out=dst, in_=src), (), [('w', i)])

        def wload(tname, idx, r0, r1, c0, c1):
            spec = (tname, tuple(idx), r0, r1, c0, c1)
            k = wcount[0]
            wcount[0] += 1
            if wspecs is None:
                recorded.append(spec)
                _wissue(k, spec)
            else:
                assert wspecs[k] == spec, (k, spec, wspecs[k])
                while wissued[0] <= min(k + WPF, len(wspecs) - 1):
                    _wissue(wissued[0], wspecs[wissued[0]])
                    wissued[0] += 1
            kc = (r1 - r0) // 128
            i = k % NWB
            return wbuf[i][:, 0:kc * 128].rearrange("p (c n) -> p c n", c=kc), ('w', i)

        S.dma('sp', 'ld', lambda: nc.sync.dma_start(out=cf[:, :], in_=cf_d[:, :]), (), ['cf'])
        S.dma('sp', 'ld', lambda: nc.sync.dma_start(out=ident[:, :], in_=id_d[:, :]), (), ['ident'])
        S.dma('pool', 'w0', lambda: nc.gpsimd.dma_start(out=cb[:, :], in_=cb_d[:, :]), (), ['cb'])
        gsrcs = [g_mix_d[0:1, :], g_mlp_d[0:1, :], g_mix_d[1:2, :], g_mlp_d[1:2, :], g_fin_d[0:1, :]]
        for gi, gs in enumerate(gsrcs):
            src = gs.rearrange("o (c p) -> p (o c)", p=128)
            S.dma('act', 'ldg', (lambda gi=gi, src=src: nc.scalar.dma_start(out=gT[:, gi, :], in_=src, allow_slow_non_contiguous=True)), (), ['gT'])

        xres = lambda c, i: ('xT', c, i)
        stgres = lambda sl: [('ys', sl, i) for i in range(NTT)]
        ares = lambda hh: [('accB', hh)] + [('ys', c, i) for c in (4 + 2 * hh, 5 + 2 * hh) for i in range(NTT)]

        def mbres(dcc, i):
            if dcc == 0:
                return [('q', i)]
            if dcc == 1:
                return [('k', i)]
            if dcc == 2:
                return [('v', i // 2), 'vones']
            return [('v', 2 + i // 2), 'vones']
        for b in range(NKB):
            sl = b % 2
            S.dma('sp', 'ld%d' % sl, (lambda b=b, sl=sl: nc.sync.dma_start(out=stg[:, sl, :], in_=x_d[128 * b:128 * b + 128, :])),
                  (), stgres(sl))
            for half in range(2):
                bank = (2 * b + half) % 4
                for cc in range(4):
                    c = 4 * half + cc
                    S.op('pe', (lambda bank=bank, cc=cc, c=c, sl=sl: nc.tensor.transpose(
                        out=PS[bank][:, 128 * cc:128 * cc + 128], in_=stg[:, sl, 128 * c:128 * c + 128], identity=ident[:, :])),
                        stgres(sl) + ['ident'], [('ps', bank)])
                eng = 'dve' if half == 0 else 'act'
                cp(eng, xT[:, 4 * half:4 * half + 4, 128 * b:128 * b + 128],
                   PS[bank][:, :].rearrange("p (a t) -> p a t", a=4),
                   [('ps', bank)], [xres(c, b // 4) for c in range(4 * half, 4 * half + 4)])

        def rmsnorm_to_hT(gidx):
            for i in range(NTT):
                tsl = slice(512 * i, 512 * i + 512)
                for c in range(NCH):
                    sq, sqr = nwh()
                    act(sq[:, :], xT[:, c, tsl], AF.Square, [xres(c, i)], [sqr])
                    mm(PS[7][:, :], cb[:, CB_ONES:CB_ONES + 128], sq[:, :], c == 0, c == NCH - 1, [sqr, 'cb'], [('ps', 7)])
                lnv, lnr = nwf()
                act(lnv[:, :], PS[7][:, :], AF.Ln, [('ps', 7)], [lnr], bias=EPS, scale=1.0 / D)
                rs, rsr = nwf()
                act(rs[:, :], lnv[:, :], AF.Exp, [lnr], [rsr], scale=-0.5)
                for c in range(NCH):
                    eng = 'dve'
                    stt(eng, hT[:, c, tsl], xT[:, c, tsl], gT[:, gidx, c:c + 1], rs[:, :], ALU.mult, ALU.mult,
                        [xres(c, i), rsr, 'gT'], [('hT', c, i)])

        hres_all = lambda i: [('hT', c, i) for c in range(NCH)]

        def project_unit(l, qoff, koff, voff, r, qscale, dmode=False):
            wq3, wqr = wload('w_in', (l,), 0, D, qoff, qoff + 128)
            wk3, wkr = wload('w_in', (l,), 0, D, koff, koff + 128)
            wv3, wvr = wload('w_in', (l,), 0, D, voff, voff + 128)
            L = S_LEN // r

            def hview(c):
                return hT[:, c, :].rearrange("p (m r) -> p r m", r=r)

            def blk_cols(v, jb):
                if r == 1:
                    return v[:, 0, 128 * jb:128 * jb + 128]
                if r == 4:
                    return v[:, jb // 4, 128 * (jb % 4):128 * (jb % 4) + 128]
                return v[:, jb, :]

            def pdst(t2d, rows, i):
                if r == 1:
                    return t2d[rows, 512 * i:512 * i + 512]
                w = 512 // r
                return t2d[rows, :].rearrange("p (cc m) -> p cc m", cc=r)[:, :, w * i:w * i + w]

            def psrc(bank, rows):
                if r == 1:
                    return PS[bank][rows, :]
                return PS[bank][rows, :].rearrange("p (mm cc) -> p cc mm", cc=r)
            hr_all = [('hT', c, i) for c in range(NCH) for i in range(NTT)]
            allq = [('q', i) for i in range(NTT)]
            allq2 = [('q2', i) for i in range(NTT)]
            allk = [('k', i) for i in range(NTT)]
            lo, hi = slice(0, 64), slice(64, 128)
            al = slice(0, 128)
            for i in range(NTT):
                tsl = slice(512 * i, 512 * i + 512)
                bq = (2 * i) % 4
                bk = (2 * i + 1) % 4
                for c in range(NCH):
                    mm(PS[bq][:, :], wq3[:, c, :], hT[:, c, tsl], c == 0, c == NCH - 1, hres_all(i) + [wqr], [('ps', bq)])
                if dmode:
                    for qq, qn, mc in ((qT, 'q', 0), (q2T, 'q2', 2), (q3T, 'q3', 1), (q4T, 'q4', 3)):
                        ts('dve' , qq[:, tsl], PS[bq][:, :], cf[:, CF_M0 + mc:CF_M0 + mc + 1], None, ALU.mult, None,
                           [('ps', bq), 'cf'], [(qn, i)])
                else:
                    wq_ = [('q', i)] if r == 1 else allq
                    wq2_ = [('q2', i)] if r == 1 else allq2
                    act(pdst(qT, lo, i), psrc(bq, lo), AF.Copy, [('ps', bq)], wq_, scale=qscale)
                    ts('dve', pdst(q2T, hi, i), psrc(bq, hi), float(qscale), None, ALU.mult, None, [('ps', bq)], wq2_)
                for c in range(NCH):
                    mm(PS[bk][:, :], wk3[:, c, :], hT[:, c, tsl], c == 0, c == NCH - 1, hres_all(i) + [wkr], [('ps', bk)])
                cp('dve', pdst(kT, al, i), psrc(bk, al), [('ps', bk)], [('k', i)] if r == 1 else allk)
            for i in range(NTT):
                hr = hr_all if r > 1 else hres_all(i)
                bv = 4 + (i % 2)
                for jj in range(4):
                    jb = 4 * i + jj
                    for c in range(NCH):
                        mm(PS[bv][:, 128 * jj:128 * jj + 128], blk_cols(hview(c), jb), wv3[:, c, :], c == 0, c == NCH - 1,
                           hr + [wvr], [('ps', bv)])
                cp('act', vA[:, 4 * i:4 * i + 4, :, 0:64],
                   PS[bv][:, :].rearrange("p (j h e) -> p j h e", j=4, h=2),
                   [('ps', bv)], [('v', i)])

        def run_pipeline(tasks, stages, offs=None):
            n = len(tasks)
            ns = len(stages)
            if offs is None:
                offs = list(range(ns))
            for step in range(n + offs[-1]):
                for si, st in enumerate(stages):
                    k = step - offs[si]
                    if 0 <= k < n:
                        st(tasks[k])

        qres = lambda i: ('q', i)
        kres = lambda j: ('k', j // 4)
        vres = lambda j: ('v', j // 4)

        def zero_acc(bank, m=128):
            mm(PS[bank][0:m, :], cb[:, CB_ZERO:CB_ZERO + m], cb[:, 0:512], True, False, ['cb'], [('ps', bank)])

        def attn_dilated(g, hp, l):
            r = DIL_R[g]
            tasks = []
            grp = 0
            for hh in range(2):
                h = 2 * hp + hh
                coef = -float(SLOPES_DIL[g][h]) * r
                for i in range(NTT):
                    accb = 5 + (grp % 3)
                    grp += 1
                    tl = []
                    if r == 1:
                        if i > 0:
                            tl.append((4 * i - 1, 0, 128, 128))
                        for m in range(4):
                            tl.append((4 * i + m, 128 * m, min(128 * m + 256, 512), 0))
                    elif r == 4:
                        for m in range(4):
                            tl.append((4 * i + m, 128 * m, min(128 * m + 256, 512), 0))
                    else:
                        for m in range(4):
                            tl.append((4 * i + m, 128 * m, 128 * m + 128, 0))
                    for ti, (j, c0, c1, dwo) in enumerate(tl):
                        tasks.append(dict(hh=hh, h=h, i=i, j=j, c0=c0, c1=c1, dwo=dwo, coef=coef, accb=accb,
                                          first=(ti == 0), last=(ti == len(tl) - 1), idx=len(tasks)))

            def st_score(t):
                sbk = t['idx'] % 5
                t['sb'] = sbk
                rows = slice(64 * t['hh'], 64 * t['hh'] + 64)
                qq = qT if t['hh'] == 0 else q2T
                mm(PS[sbk][:, t['c0']:t['c1']], kT[:, 128 * t['j']:128 * t['j'] + 128], qq[:, 512 * t['i'] + t['c0']:512 * t['i'] + t['c1']],
                   True, True, [qres(t['i']), ('q2', t['i']), kres(t['j'])], [('ps', sbk)])

            def st_elem(t):
                sbk = t['sb']
                c0, c1 = t['c0'], t['c1']
                w = c1 - c0
                tmp, tr = nwf()
                stt('dve', tmp[:, c0:c1], cb[:, CB_DW + t['dwo']:CB_DW + t['dwo'] + w], t['coef'], PS[sbk][:, c0:c1], ALU.mult, ALU.add,
                    [('ps', sbk), 'cb'], [tr])
                pt, pr = nwh()
                act(pt[:, c0:c1], tmp[:, c0:c1], AF.Exp, [tr], [pr])
                t['pt'], t['pr'] = pt, pr

            def st_pv(t):
                c0, c1 = t['c0'], t['c1']
                for cc in range(c0, c1, 128):
                    mm(PS[t['accb']][:, cc:cc + 128], vA[:, t['j'], t['hh'], :], t['pt'][:, cc:cc + 128], (t['first'] and cc == c0),
                       (t['last'] and cc + 128 >= c1), [t['pr'], vres(t['j']), 'vones'], [('ps', t['accb'])], sgc=True)
                if t['last']:
                    i, hh = t['i'], t['hh']
                    av = accB[:, hh, :].rearrange("p (m r) -> p r m", r=r)
                    if r == 1:
                        dst = av[:, 0, 512 * i:512 * i + 512]
                        src = PS[t['accb']][:, :]
                    elif r == 4:
                        dst = av[:, i, :]
                        src = PS[t['accb']][:, :]
                    else:
                        dst = av[:, 4 * i:4 * i + 4, :]
                        src = PS[t['accb']][:, :].rearrange("p (a t) -> p a t", a=4)
                    if g == 0:
                        cp('act', dst, src, [('ps', t['accb'])], ares(hh))
                    else:
                        tmpc, tcr = nwf()
                        cp('act', tmpc[:, :], PS[t['accb']][:, :], [('ps', t['accb'])], [tcr])
                        srcs = tmpc[:, :] if r < 16 else tmpc[:, :].rearrange("p (a t) -> p a t", a=4)
                        tt('pool', dst, srcs, dst, ALU.add, [tcr] + ares(hh), ares(hh))
            run_pipeline(tasks, [st_score, st_elem, st_pv], [0, 2, 4])

        def finalize_dilated(hp):
            for hh in range(2):
                for i in range(NTT):
                    tsl = slice(512 * i, 512 * i + 512)
                    rc, rr = nwf()
                    recip(rc[0:64, :], accB[64:128, hh, tsl], ares(hh), [rr])
                    tt('dve', ysT[64 * hh:64 * hh + 64, 2 + hp, tsl], accB[0:64, hh, tsl], rc[0:64, :], ALU.mult,
                       ares(hh) + [rr], [('ys', 2 + hp, i)])

        def attn_stick(hp):
            tasks = []
            grp = 0
            Racc, Rr = WF[0], ('wf', 0)
            for hh in range(2):
                for i in range(NTT):
                    accb = 5 + (grp % 2)
                    grp += 1
                    js = list(range(4 * i + 3, -1, -1))
                    for ti, j in enumerate(js):
                        m = j - 4 * i
                        c0 = 128 * m if m > 0 else 0
                        tasks.append(dict(hh=hh, i=i, j=j, m=m, c0=c0, accb=accb, first=(ti == 0), last=(ti == len(js) - 1), idx=len(tasks)))
            rb = [None]

            def st_z(t):
                zb = t['idx'] % 3
                t['zb'] = zb
                rows = slice(64 * t['hh'], 64 * t['hh'] + 64)
                c0 = t['c0']
                qq = qT if t['hh'] == 0 else q2T
                mm(PS[zb][:, c0:512], kT[:, 128 * t['j']:128 * t['j'] + 128], qq[:, 512 * t['i'] + c0:512 * t['i'] + 512],
                   True, True, [qres(t['i']), ('q2', t['i']), kres(t['j'])], [('ps', zb)])

            def st_sp(t):
                c0 = t['c0']
                zb = t['zb']
                act(PS[zb][:, c0:512], PS[zb][:, c0:512], AF.Exp, [('ps', zb)], [('ps', zb)])
                sp, spr = nwh()
                act(sp[:, c0:512], PS[zb][:, c0:512], AF.Ln, [('ps', zb)], [spr], bias=1.0)
                if t['m'] >= 0:
                    tt('pool', sp[:, c0:c0 + 128], sp[:, c0:c0 + 128], cb[:, CB_TRIS:CB_TRIS + 128], ALU.mult, [spr, 'cb'], [spr])
                t['sp'], t['spr'] = sp, spr

            def st_l(t):
                c0 = t['c0']
                lb = 3 + (t['idx'] % 2)
                t['lb'] = lb
                rows = slice(64 * t['hh'], 64 * t['hh'] + 64)
                if t['first']:
                    memset('pool', Racc[:, :], 0.0, [Rr])
                qq = qT if t['hh'] == 0 else q2T
                mm(PS[lb][:, c0:512], kT[:, 128 * t['j']:128 * t['j'] + 128], qq[:, 512 * t['i'] + c0:512 * t['i'] + 512],
                   True, False, [qres(t['i']), ('q2', t['i']), kres(t['j'])], [('ps', lb)])
                mm(PS[lb][:, c0:512], cb[:, CB_NTRI:CB_NTRI + 128], t['sp'][:, c0:512], False, t['first'], [t['spr'], 'cb'], [('ps', lb)])
                if not t['first']:
                    rbt, rbr = rb[0]
                    mm(PS[lb][:, c0:512], cb[:, CB_NONES:CB_NONES + 128], rbt[:, c0:512], False, True, [rbr, 'cb'], [('ps', lb)])
                if not t['last']:
                    tt('dve', Racc[:, c0:512], Racc[:, c0:512], t['sp'][:, c0:512], ALU.add, [Rr, t['spr']], [Rr])
                    nb, nbr = nwh()
                    cp('dve', nb[:, :], Racc[:, :], [Rr], [nbr])
                    rb[0] = (nb, nbr)

            def st_a(t):
                c0 = t['c0']
                lb = t['lb']
                pt, pr = nwh()
                act(pt[:, c0:512], PS[lb][:, c0:512], AF.Exp, [('ps', lb)], [pr])
                if t['m'] >= 0:
                    tt('pool', pt[:, c0:c0 + 128], pt[:, c0:c0 + 128], cb[:, CB_TRIS:CB_TRIS + 128], ALU.mult, [pr, 'cb'], [pr])
                t['pt'], t['pr'] = pt, pr

            def st_pv(t):
                c0 = t['c0']
                mm(PS[t['accb']][:, c0:512], vA[:, t['j'], t['hh'], :], t['pt'][:, c0:512], t['first'], t['last'],
                   [t['pr'], vres(t['j']), 'vones'], [('ps', t['accb'])], sgc=True)
                if t['last']:
                    i, hh = t['i'], t['hh']
                    cp('dve', ysT[64 * hh:64 * hh + 64, hp, 512 * i:512 * i + 512], PS[t['accb']][0:64, :],
                       [('ps', t['accb'])], [('ys', hp, i)])

            def nwf_a():
                wfc[0] += 1
                i = 1 + (wfc[0] % (NWF - 1))
                return WF[i], ('wf', i)
            run_pipeline(tasks, [st_z, st_sp, st_l, st_a, st_pv])

        def fox_prep(l):
            wf3, wfr = wload('w_in', (l,), 0, D, OFF_F, OFF_F + 4)
            S.dma('sp', 'ld', lambda: nc.sync.dma_start(out=bfb[:, :], in_=bf_d[l:l + 1, :].broadcast_to([128, 4])), (), ['bfb'])
            for b in range(NKB):
                for c in range(NCH):
                    mm(PS[7][:, 4 * b:4 * b + 4], hT[:, c, 128 * b:128 * b + 128], wf3[:, c, 0:4], c == 0, c == NCH - 1,
                       hres_all(b // 4) + [wfr], [('ps', 7)])
            t0, t0r = nwf()
            tt('dve', t0[:, 0:64].rearrange("p (b h) -> p b h", h=4), PS[7][:, 0:64].rearrange("p (b h) -> p b h", h=4),
               bfb[:, :].unsqueeze(1).broadcast_to([128, NKB, 4]), ALU.add, [('ps', 7), 'bfb'], [t0r])
            t1, t1r = nwf()
            act(t1[:, 0:64], t0[:, 0:64], AF.Exp, [t0r], [t1r], scale=-1.0)
            t2, t2r = nwf()
            act(t2[:, 0:64], t1[:, 0:64], AF.Ln, [t1r], [t2r], bias=1.0)
            ts('dve', logf[:, :, :], t2[:, 0:64].rearrange("p (b h) -> p b h", h=4), -1.0, None, ALU.mult, None, [t2r], ['logf'])
            lf2 = logf[:, :, :].rearrange("p b h -> p (b h)")
            mm(PS[7][:, 64:128], cf[:, CF_TRI:CF_TRI + 128], lf2, True, True, ['logf', 'cf'], [('ps', 7)])
            mm(PS[7][:, 128:192], cf[:, CF_ONES:CF_ONES + 128], lf2, True, True, ['logf', 'cf'], [('ps', 7)])
            tb, tbr = nwf()
            tb3 = tb[:, 0:64].rearrange("p (b h) -> p b h", h=4)
            cp('dve', tb[:, 0:64], PS[7][:, 128:192], [('ps', 7)], [tbr])
            cp('dve', cend[:, 0, :], tb3[:, 0, :], [tbr], ['cend'])
            for j in range(1, NKB):
                tt('dve', cend[:, j, :], cend[:, j - 1, :], tb3[:, j, :], ALU.add, [tbr, 'cend'], ['cend'])
            cp('dve', cumtok[:, 0, :], PS[7][:, 64:68], [('ps', 7)], ['cumtok'])
            tt('dve', cumtok[:, 1:NKB, :], PS[7][:, 68:128].rearrange("p (b h) -> p b h", h=4), cend[:, 0:NKB - 1, :], ALU.add,
               [('ps', 7), 'cend'], ['cumtok'])
            for i in range(NTT):
                nj = 4 * i + 4
                for h in range(4):
                    if i == 0:
                        ts('dve', FB[:, i, 0:nj, h], cumtok[:, 0:nj, h], -1.0, None, ALU.mult, None, ['cumtok'], ['FB'])
                    else:
                        ts('dve', FB[:, i, 0:nj, h], cumtok[:, 0:nj, h], -1.0, cend[:, 4 * i - 1, h:h + 1], ALU.mult, ALU.add,
                           ['cumtok', 'cend'], ['FB'])

        def attn_fox(hp):
            tasks = []
            grp = 0
            CBs = [(WF[0], ('wf', 0)), (CB2, 'CB2')]
            groups = []

            def nwf_c():
                wfc[0] += 1
                i = 1 + (wfc[0] % (NWF - 1))
                return WF[i], ('wf', i)
            for hh in range(2):
                for i in range(NTT):
                    accb = 5 + (grp % 2)
                    gidx = grp
                    groups.append((2 * hp + hh, i))
                    grp += 1
                    js = list(range(4 * i, 4 * i + 4)) + list(range(0, 4 * i))
                    for ti, j in enumerate(js):
                        m = j - 4 * i
                        tasks.append(dict(hh=hh, h=2 * hp + hh, g=gidx, i=i, j=j, m=m, c0=(128 * m if m > 0 else 0), accb=accb,
                                          first=(ti == 0), last=(ti == len(js) - 1), idx=len(tasks)))

            def prep_group(g):
                h, i = groups[g]
                CBt, CBr = CBs[g % 2]
                for bb in range(4):
                    b = 4 * i + bb
                    mm(PS[7][0:1, 128 * bb:128 * bb + 128], logf[:, b, h:h + 1], cf[:, CF_TRI:CF_TRI + 128], bb == 0, bb == 3,
                       ['logf', 'cf'], [('ps', 7)])
                    if bb < 3:
                        mm(PS[7][0:1, 128 * bb + 128:512], logf[:, b, h:h + 1], cf[:, CF_ONES:CF_ONES + (384 - 128 * bb)], False, False,
                           ['logf', 'cf'], [('ps', 7)])
                cp('act', clr[0:1, :], PS[7][0:1, :], [('ps', 7)], ['clr'])
                mm(PS[7][:, :], cf[0:1, CF_ONES:CF_ONES + 128], clr[0:1, :], True, True, ['clr', 'cf'], [('ps', 7)])
                cp('act', CBt[:, :], PS[7][:, :], [('ps', 7)], [CBr])

            def st_score(t):
                sbk = t['idx'] % 5
                t['sb'] = sbk
                rows = slice(64 * t['hh'], 64 * t['hh'] + 64)
                c0 = t['c0']
                qq = qT if t['hh'] == 0 else q2T
                mm(PS[sbk][:, c0:512], kT[:, 128 * t['j']:128 * t['j'] + 128], qq[:, 512 * t['i'] + c0:512 * t['i'] + 512],
                   True, True, [qres(t['i']), ('q2', t['i']), kres(t['j'])], [('ps', sbk)])

            def st_elem(t):
                sbk = t['sb']
                c0 = t['c0']
                if t['first']:
                    if t['g'] == 0:
                        prep_group(0)
                    if t['g'] + 1 < len(groups):
                        prep_group(t['g'] + 1)
                CBt, CBr = CBs[t['g'] % 2]
                tmp, tr = nwf_c()
                fbc = FB[:, t['i'], t['j'], t['h']:t['h'] + 1]
                stt('dve', tmp[:, c0:512], PS[sbk][:, c0:512], fbc, CBt[:, c0:512], ALU.add, ALU.add, [('ps', sbk), 'FB', CBr], [tr])
                if t['m'] >= 0:
                    tt('pool', tmp[:, c0:c0 + 128], tmp[:, c0:c0 + 128], cf[:, CF_NEGM:CF_NEGM + 128], ALU.add, [tr, 'cf'], [tr])
                pt, pr = nwh()
                act(pt[:, c0:512], tmp[:, c0:512], AF.Exp, [tr], [pr])
                t['pt'], t['pr'] = pt, pr

            def st_pv(t):
                c0 = t['c0']
                mm(PS[t['accb']][:, c0:512], vA[:, t['j'], t['hh'], :], t['pt'][:, c0:512], t['first'], t['last'],
                   [t['pr'], vres(t['j']), 'vones'], [('ps', t['accb'])])
                if t['last']:
                    i, hh = t['i'], t['hh']
                    rc, rr = nwf_c()
                    recip(rc[0:64, :], PS[t['accb']][64:128, :], [('ps', t['accb'])], [rr])
                    tt('dve', ysT[64 * hh:64 * hh + 64, 4 + hp, 512 * i:512 * i + 512], PS[t['accb']][0:64, :], rc[0:64, :], ALU.mult,
                       [('ps', t['accb']), rr], [('ys', 4 + hp, i)])
            run_pipeline(tasks, [st_score, st_elem, st_pv], [0, 2, 4])

        def diff_prep(l, lam_init):
            for k, src in enumerate([lq1_d, lk1_d, lq2_d, lk2_d]):
                S.dma('sp', 'ld', (lambda k=k, src=src: nc.sync.dma_start(out=ltab[:, k, :], in_=src[l:l + 1, :].broadcast_to([128, 32]))), (), ['ltab'])
            S.dma('sp', 'ld', lambda: nc.sync.dma_start(out=lsm[0:64, 8:9], in_=gd_d[l:l + 1, :].rearrange("o d -> d o"), allow_slow_non_contiguous=True), (), ['lsm8'])
            S.dma('sp', 'ld', lambda: nc.sync.dma_start(out=lsm[64:128, 8:9], in_=gd_d[l:l + 1, :].rearrange("o d -> d o"), allow_slow_non_contiguous=True), (), ['lsm8'])
            pr1, pr1r = nwf()
            tt('dve', pr1[:, 0:32], ltab[:, 0, :], ltab[:, 1, :], ALU.mult, ['ltab'], [pr1r])
            S.op('dve', lambda: nc.vector.reduce_sum(out=lsm[:, 0:1], in_=pr1[:, 0:32], axis=AX.X), [pr1r], ['lsm0'])
            pr2, pr2r = nwf()
            tt('dve', pr2[:, 0:32], ltab[:, 2, :], ltab[:, 3, :], ALU.mult, ['ltab'], [pr2r])
            S.op('dve', lambda: nc.vector.reduce_sum(out=lsm[:, 1:2], in_=pr2[:, 0:32], axis=AX.X), [pr2r], ['lsm0'])
            act(lsm[:, 2:4], lsm[:, 0:2], AF.Exp, ['lsm0'], ['lsm2'])
            tt('dve', lsm[:, 4:5], lsm[:, 3:4], lsm[:, 2:3], ALU.subtract, ['lsm2'], ['lsm4'])
            ts('dve', lsm[:, 5:6], lsm[:, 4:5], -float(lam_init), None, ALU.add, None, ['lsm4'], ['neglam'])
            ts('dve', lsm[:, 9:10], lsm[:, 8:9], float(1.0 - lam_init), None, ALU.mult, None, ['lsm8'], ['gdcol'])

        def attn_diff(hp):
            tasks = []
            Yt, Yr = WF[0], ('wf', 0)

            def nwf_d():
                wfc[0] += 1
                i = 1 + (wfc[0] % (NWF - 1))
                return WF[i], ('wf', i)
            grp = 0
            for i in reversed(range(NTT)):
                for hh in range(2):
                    for cm in range(2):
                        accb = 3 + (grp % 4)
                        grp += 1
                        js = list(range(4 * i, 4 * i + 4)) + list(range(0, 4 * i))
                        for ti, j in enumerate(js):
                            m = j - 4 * i
                            tasks.append(dict(hh=hh, h=2 * hp + hh, cm=cm, i=i, j=j, m=m, c0=(128 * m if m > 0 else 0), accb=accb,
                                              first=(ti == 0), last=(ti == len(js) - 1), idx=len(tasks)))
            accs = {}

            def st_score(t):
                sbk = t['idx'] % 3
                t['sb'] = sbk
                rows = slice(64 * t['hh'], 64 * t['hh'] + 64)
                c0 = t['c0']
                qq = ((qT, q3T), (q2T, q4T))[t['hh']][t['cm']]
                mm(PS[sbk][:, c0:512], kT[:, 128 * t['j']:128 * t['j'] + 128], qq[:, 512 * t['i'] + c0:512 * t['i'] + 512],
                   True, True, [qres(t['i']), ('q2', t['i']), ('q3', t['i']), ('q4', t['i']), kres(t['j'])], [('ps', sbk)])

            def st_elem(t):
                sbk = t['sb']
                c0 = t['c0']
                pt, pr = nwh()
                mmi = t['j'] - 4 * t['i'] + 12
                col = CF_DB + 16 * t['h'] + mmi
                act(pt[:, c0:512], PS[sbk][:, c0:512], AF.Exp, [('ps', sbk), 'cf'], [pr], bias=cf[:, col:col + 1], scale=float(32 ** -0.5))
                if t['m'] >= 0:
                    tt('pool', pt[:, c0:c0 + 128], pt[:, c0:c0 + 128], cb[:, CB_TRII:CB_TRII + 128], ALU.mult, [pr, 'cb'], [pr])
                t['pt'], t['pr'] = pt, pr

            pending = []

            def st_pv(t):
                c0 = t['c0']
                while pending and pending[0][0] <= t['idx']:
                    pending.pop(0)[1]()
                mm(PS[t['accb']][:, c0:512], vA[:, t['j'], t['hh'], :], t['pt'][:, c0:512], t['first'], t['last'],
                   [t['pr'], vres(t['j']), 'vones'], [('ps', t['accb'])])
                if t['last']:
                    accs[(t['i'], t['hh'], t['cm'])] = t['accb']
                    if t['cm'] == 1:
                        i, hh = t['i'], t['hh']
                        a0 = accs[(i, hh, 0)]
                        a1 = accs[(i, hh, 1)]
                        r0, r0r = nwf_d()
                        recip_dve(r0[0:64, :], PS[a0][64:128, :], [('ps', a0)], [r0r])
                        r1, r1r = nwf_d()
                        recip_dve(r1[0:64, :], PS[a1][64:128, :], [('ps', a1)], [r1r])
                        y0, y0r = nwf_d()
                        tt('dve', y0[0:64, :], PS[a0][0:64, :], r0[0:64, :], ALU.mult, [('ps', a0), r0r], [y0r])
                        tt('dve', r1[0:64, :], PS[a1][0:64, :], r1[0:64, :], ALU.mult, [('ps', a1), r1r], [r1r])
                        stt('dve', Yt[64 * hh:64 * hh + 64, :], r1[0:64, :], lsm[0:64, 5:6], y0[0:64, :], ALU.mult, ALU.add,
                            [r1r, y0r, 'neglam'], [Yr])
                        if hh == 1:
                            def f2(i=i):
                                sq, sqr = nwh()
                                act(sq[:, :], Yt[:, :], AF.Square, [Yr], [sqr])
                                mm(PS[7][:, :], cb[:, CB_BD:CB_BD + 128], sq[:, :], True, True, [sqr, 'cb'], [('ps', 7)])
                                lnv, lnr = nwf_d()
                                act(lnv[:, :], PS[7][:, :], AF.Ln, [('ps', 7)], [lnr], bias=EPS, scale=1.0 / 64)
                                rs, rsr = nwf_d()
                                act(rs[:, :], lnv[:, :], AF.Exp, [lnr], [rsr], scale=-0.5)
                                stt('dve', ysT[:, 6 + hp, 512 * i:512 * i + 512], Yt[:, :], lsm[:, 9:10], rs[:, :], ALU.mult, ALU.mult,
                                    [Yr, rsr, 'gdcol'], [('ys', 6 + hp, i)])
                            pending.append((t['idx'] + 6, f2))
            run_pipeline(tasks, [st_score, st_elem, st_pv])

            def flush():
                while pending:
                    pending.pop(0)[1]()
            return flush

        def gate_and_out(l):
            for hf in range(2):
                for dcc in range(4):
                    dc = 4 * hf + dcc
                    maccs = [None] * NTT
                    for n in range(4):
                        wg3, wgr = wload('w_in', (l,), 0, D, OFF_G + n * D + 128 * dc, OFF_G + n * D + 128 * dc + 128)
                        wb3, wbr = wload('w_branch', (l, n), 0, 256, 128 * dc, 128 * dc + 128)
                        for i in range(NTT):
                            tsl = slice(512 * i, 512 * i + 512)
                            gb = 2 * i
                            bb = 2 * i + 1
                            for c in range(NCH):
                                mm(PS[gb][:, :], wg3[:, c, :], hT[:, c, tsl], c == 0, c == NCH - 1, hres_all(i) + [wgr], [('ps', gb)])
                            sg, sgr = nwf2()
                            act(sg[:, :], PS[gb][:, :], AF.Sigmoid, [('ps', gb)], [sgr])
                            for cc in range(2):
                                mm(PS[bb][:, :], wb3[:, cc, :], ysT[:, 2 * n + cc, tsl], cc == 0, cc == 1,
                                   [('ys', 2 * n + cc, i), wbr], [('ps', bb)])
                            if n == 0:
                                maccs[i] = (WF[i], ('wf', i))
                                tt('dve', WF[i][:, :], sg[:, :], PS[bb][:, :], ALU.mult, [sgr, ('ps', bb)], [('wf', i)])
                            else:
                                tt('dve', sg[:, :], sg[:, :], PS[bb][:, :], ALU.mult, [sgr, ('ps', bb)], [sgr])
                                if n < 3:
                                    tt('pool', WF[i][:, :], WF[i][:, :], sg[:, :], ALU.add, [('wf', i), sgr], [('wf', i)])
                                else:
                                    tt('pool', MB[:, dcc, tsl], WF[i][:, :], sg[:, :], ALU.add, [('wf', i), sgr], mbres(dcc, i))
                for dco in range(NCH):
                    wo3, wor = wload('w_out', (l,), 512 * hf, 512 * hf + 512, 128 * dco, 128 * dco + 128)
                    for i in range(NTT):
                        tsl = slice(512 * i, 512 * i + 512)
                        ob = (dco * NTT + i) % 8
                        for dcc in range(4):
                            mm(PS[ob][:, :], wo3[:, dcc, :], MB[:, dcc, tsl], dcc == 0, dcc == 3, mbres(dcc, i) + [wor], [('ps', ob)])
                        tt('dve', xT[:, dco, tsl], PS[ob][:, :], xT[:, dco, tsl], ALU.add, [('ps', ob), xres(dco, i)], [xres(dco, i)])

        def nwf2():
            wfc[0] += 1
            i = 4 + (wfc[0] % 2)
            return WF[i], ('wf', i)

        def mlp(l):
            upT = ysT
            for f in range(4):
                for fc in range(NCH):
                    col = 1024 * f + 128 * fc
                    wu3, wur = wload('w_up', (l,), 0, D, col, col + 128)
                    for i in range(NTT):
                        tsl = slice(512 * i, 512 * i + 512)
                        ub = (fc * NTT + i) % 8
                        for c in range(NCH):
                            mm(PS[ub][:, :], wu3[:, c, :], hT[:, c, tsl], c == 0, c == NCH - 1, hres_all(i) + [wur], [('ps', ub)])
                        rl, rlr = nwf()
                        act(rl[:, :], PS[ub][:, :], AF.Relu, [('ps', ub)], [rlr])
                        act(upT[:, fc, tsl], rl[:, :], AF.Square, [rlr], [('ys', fc, i)])
                for dco in range(NCH):
                    wd3, wdr = wload('w_down', (l,), 1024 * f, 1024 * f + 1024, 128 * dco, 128 * dco + 128)
                    for i in range(NTT):
                        tsl = slice(512 * i, 512 * i + 512)
                        db = (dco * NTT + i) % 8
                        for fc in range(NCH):
                            mm(PS[db][:, :], wd3[:, fc, :], upT[:, fc, tsl], fc == 0, fc == NCH - 1, [('ys', fc, i), wdr], [('ps', db)])
                        tt('dve', xT[:, dco, tsl], PS[db][:, :], xT[:, dco, tsl], ALU.add, [('ps', db), xres(dco, i)], [xres(dco, i)])

        try:
            chk('load')
            for l in layers:
                lam_init = 0.8 - 0.6 * math.exp(-0.3 * l)
                rmsnorm_to_hT(2 * l)
                chk('norm')
                memset('pool', vA[:, :, :, 64:128], 1.0, ['vones'])
                memset('pool', qT[64:128, :], 0.0, [('q', i) for i in range(NTT)])
                memset('pool', q2T[0:64, :], 0.0, [('q2', i) for i in range(NTT)])
                for hp in range(2):
                    for g in range(3):
                        project_unit(l, OFF_QB + g * 256 + hp * 128, OFF_KB + g * 256 + hp * 128, OFF_VB + g * 256 + hp * 128, DIL_R[g], 0.125)
                        chk('projB%d' % g)
                        attn_dilated(g, hp, l)
                        chk('attnB%d' % g)
                    finalize_dilated(hp)
                    chk('finB')
                for hp in range(2):
                    project_unit(l, OFF_QA + hp * 128, OFF_KA + hp * 128, OFF_VA + hp * 128, 1, 0.125)
                    attn_stick(hp)
                    chk('A')
                fox_prep(l)
                chk('foxprep')
                for hp in range(2):
                    project_unit(l, OFF_QC + hp * 128, OFF_KC + hp * 128, OFF_VC + hp * 128, 1, 0.125)
                    attn_fox(hp)
                    chk('C')
                diff_prep(l, lam_init)
                chk('diffprep')
                carry = None
                for hp in range(2):
                    project_unit(l, OFF_QD + hp * 128, OFF_KD + hp * 128, OFF_VD + hp * 128, 1, 1.0, dmode=True)
                    if carry is not None:
                        carry()
                    chk('projD')
                    carry = attn_diff(hp)
                    chk('D')
                carry()
                if dbg == 'ys':
                    break
                gate_and_out(l)
                chk('gate')
                rmsnorm_to_hT(2 * l + 1)
                mlp(l)
                chk('mlp')
        except _Stop:
            pass

        if dbg == 'ys':
            for c in range(NCH):
                S.dma('pool', 'dbg', (lambda c=c: nc.gpsimd.dma_start(out=dbg_d[:, S_LEN * c:S_LEN * c + S_LEN], in_=ysT[:, c, :])),
                      [('ys', c, i) for i in range(NTT)], [])

        if final_norm:
            for i in range(NTT):
                tsl = slice(512 * i, 512 * i + 512)
                for c in range(NCH):
                    sq, sqr = nwh()
                    act(sq[:, :], xT[:, c, tsl], AF.Square, [xres(c, i)], [sqr])
                    mm(PS[7][:, :], cb[:, CB_ONES:CB_ONES + 128], sq[:, :], c == 0, c == NCH - 1, [sqr, 'cb'], [('ps', 7)])
                lnv, lnr = nwf()
                act(lnv[:, :], PS[7][:, :], AF.Ln, [('ps', 7)], [lnr], bias=EPS, scale=1.0 / D)
                rs, rsr = nwf()
                act(rs[:, :], lnv[:, :], AF.Exp, [lnr], [rsr], scale=-0.5)
                for c in range(NCH):
                    eng = 'dve'
                    stt(eng, xT[:, c, tsl], xT[:, c, tsl], gT[:, 4, c:c + 1], rs[:, :], ALU.mult, ALU.mult,
                        [xres(c, i), rsr, 'gT'], [xres(c, i)])
        for b in range(NKB):
            sl = b % 2
            for half in range(2):
                bank = (2 * b + half) % 4
                for cc in range(4):
                    c = 4 * half + cc
                    S.op('pe', (lambda bank=bank, cc=cc, c=c, b=b: nc.tensor.transpose(
                        out=PS[bank][:, 128 * cc:128 * cc + 128], in_=xT[:, c, 128 * b:128 * b + 128], identity=ident[:, :])),
                        [xres(c, b // 4), 'ident'], [('ps', bank)])
                eng = 'dve' if half == 0 else 'act'
                cp(eng, stg[:, sl, 512 * half:512 * half + 512], PS[bank][:, :], [('ps', bank)], stgres(sl))
            S.dma('sp', 'st%d' % sl, (lambda b=b, sl=sl: nc.sync.dma_start(out=y_d[128 * b:128 * b + 128, :], in_=stg[:, sl, :])),
                  stgres(sl), [])

        with nc.Block() as block:
            def emit(engname, e):
                for (fn, waits, sk, amt) in S.streams[engname]:
                    for (k, v) in waits:
                        e.wait_ge(sems[k], v)
                    ins = fn()
                    ins.then_inc(sems[sk], amt)

            @block.tensor
            def _(e):
                emit('pe', e)

            @block.scalar
            def _(e):
                emit('act', e)

            @block.vector
            def _(e):
                emit('dve', e)

            @block.gpsimd
            def _(e):
                emit('pool', e)
                if 'dbg' in S.dcnt:
                    e.wait_ge(sems['dbg'], S.dcnt['dbg'])

            @block.sync
            def _(e):
                emit('sp', e)
                e.wait_ge(sems['st0'], S.dcnt['st0'])
                e.wait_ge(sems['st1'], S.dcnt['st1'])
    nc._wspecs = recorded
    return nc


_CONSTS = None


def _run(layers, final_norm, xin, inputs, dbg=None, stop_stage=None):
    global _CONSTS
    if _CONSTS is None:
        _CONSTS = _host_consts()
    cfh, cbh, identh = _CONSTS
    nc0 = build_program(layers, final_norm, dbg=dbg, stop_stage=stop_stage)
    nc = build_program(layers, final_norm, dbg=dbg, stop_stage=stop_stage, wspecs=list(nc0._wspecs))
    f = lambda a: np.ascontiguousarray(np.asarray(a, dtype=np.float32))
    shared = {
        "w_in": f(inputs["w_in"]), "w_branch": f(inputs["w_branch"]), "w_out": f(inputs["w_out"]),
        "w_up": f(inputs["w_up"]), "w_down": f(inputs["w_down"]),
        "mix_norm_g": f(inputs["mix_norm_g"]), "mlp_norm_g": f(inputs["mlp_norm_g"]),
        "final_norm_g": f(inputs["final_norm_g"]).reshape(1, D),
        "b_forget": f(inputs["b_forget"]),
        "lambda_q1": f(inputs["lambda_q1"]), "lambda_k1": f(inputs["lambda_k1"]),
        "lambda_q2": f(inputs["lambda_q2"]), "lambda_k2": f(inputs["lambda_k2"]),
        "diff_norm_g": f(inputs["diff_norm_g"]),
        "cf": cfh, "cb": cbh, "ident": identh,
    }
    n = xin.shape[0]
    in_maps = []
    for b in range(n):
        m = dict(shared)
        m["x"] = np.ascontiguousarray(xin[b])
        in_maps.append(m)
    res = run_bass_kernel_spmd(nc, in_maps, core_ids=list(range(n)))
    return res


def kernel(**inputs):
    x = np.asarray(inputs["x"], dtype=np.float32)
    res = _run([0, 1], True, x, inputs)
    return np.stack([r["y"] for r in res.results], axis=0).astype(np.float32)
```
